# Optimizing a Trainium2 kernel written in Bass

```python
import jax, jax.numpy as jnp
from jax import lax
import numpy as np

D_MODEL = 1024
BATCH = 8
SEQ = 4096
DEPTH = 1

CHUNK = 64
N_MEM = 256
D_MIX = D_MODEL
D_CONV = 3 * D_MIX // 8
D_POOL = 3 * D_MIX // 8
D_ATT = D_MIX - D_CONV - D_POOL
N_MEM_HEADS = 4
HEAD_DIM = D_ATT // N_MEM_HEADS
CONV_WIDTH = 3
POOL_WINDOWS = (2, 4, 8, 16)
N_POOL_GROUPS = len(POOL_WINDOWS)
POOL_GROUP = D_POOL // N_POOL_GROUPS
D_IN_PROJ = 4 * D_CONV + 2 * D_POOL + 2 * D_ATT
EPS = 1e-6

kernel_name = "hybrid_conv_pool_memattn_block"


def rmsnorm(x, w):
    xf = x.astype(jnp.float32)
    y = xf * lax.rsqrt(jnp.mean(xf * xf, axis=-1, keepdims=True) + EPS)
    return (y * w.astype(jnp.float32)).astype(x.dtype)


def short_conv_causal(u, w):
    s = u.shape[1]
    up = jnp.pad(u, ((0, 0), (CONV_WIDTH - 1, 0), (0, 0)))
    y = up[:, 0:s] * w[0]
    for k in range(1, CONV_WIDTH):
        y = y + up[:, k:k + s] * w[k]
    return y


def multiscale_pool(u, pool_w, pool_scale):
    b, s, _ = u.shape
    ug = u.reshape(b, s, N_POOL_GROUPS, POOL_GROUP)
    ugf = ug.astype(jnp.float32)
    cs = jnp.cumsum(ugf, axis=1)
    t = jnp.arange(s)
    outs = []
    for g, win in enumerate(POOL_WINDOWS):
        c = cs[:, :, g]
        shifted = jnp.pad(c, ((0, 0), (win, 0), (0, 0)))[:, :s]
        cnt = jnp.minimum(t + 1, win).astype(jnp.float32)[:, None]
        outs.append((c - shifted) / cnt)
    pooled = jnp.stack(outs, axis=2)
    diff = (pooled - ugf).astype(u.dtype)
    mixed = jnp.einsum('bsgc,gcd->bsgd', diff, pool_w)
    return mixed.reshape(b, s, D_POOL) * pool_scale


def memory_attention(q, mem_n, w_kv):
    b, s, _ = q.shape
    m = mem_n.shape[1]
    kv = mem_n @ w_kv
    k, v = jnp.split(kv, 2, axis=-1)
    qh = q.reshape(b, s, N_MEM_HEADS, HEAD_DIM)
    kh = k.reshape(b, m, N_MEM_HEADS, HEAD_DIM)
    vh = v.reshape(b, m, N_MEM_HEADS, HEAD_DIM)
    scores = jnp.einsum('bshd,bmhd->bhsm', qh, kh).astype(jnp.float32) * (HEAD_DIM ** -0.5)
    probs = jax.nn.softmax(scores, axis=-1).astype(q.dtype)
    out = jnp.einsum('bhsm,bmhd->bshd', probs, vh)
    return out.reshape(b, s, D_ATT)


def hybrid_layer(x, mem, pre_w, mem_norm_w, w_in, conv_w, pool_w, pool_scale, w_kv, w_out, post_w):
    h = rmsnorm(x, pre_w)
    proj = h @ w_in
    splits = np.cumsum([D_CONV, D_CONV, D_CONV, D_CONV, D_POOL, D_POOL, D_ATT]).tolist()
    xc, bg, cg, gc, xp, gp, q, ga = jnp.split(proj, splits, axis=-1)
    y_conv = bg * short_conv_causal(cg * xc, conv_w)
    y_pool = multiscale_pool(xp, pool_w, pool_scale)
    y_att = memory_attention(q, rmsnorm(mem, mem_norm_w), w_kv)
    y = jnp.concatenate([y_conv * jax.nn.silu(gc),
                         y_pool * jax.nn.silu(gp),
                         y_att * jax.nn.silu(ga)], axis=-1) @ w_out
    return x + rmsnorm(y, post_w)


def setup_inputs(seed: int = 0) -> dict:
    key = jax.random.key(seed)
    ks = jax.random.split(key, 12)
    f32 = jnp.float32
    x = jax.random.normal(ks[0], (BATCH, SEQ, D_MODEL), f32)
    mem = jax.random.normal(ks[1], (BATCH, N_MEM, D_MODEL), f32)
    pre_norm_w = 1.0 + 0.02 * jax.random.normal(ks[2], (DEPTH, D_MODEL), f32)
    mem_norm_w = 1.0 + 0.02 * jax.random.normal(ks[3], (DEPTH, D_MODEL), f32)
    w_in = jax.random.normal(ks[4], (DEPTH, D_MODEL, D_IN_PROJ), f32) * D_MODEL ** -0.5
    conv_w = jax.random.normal(ks[5], (DEPTH, CONV_WIDTH, D_CONV), f32) * CONV_WIDTH ** -0.5
    pool_w = jax.random.normal(ks[6], (DEPTH, N_POOL_GROUPS, POOL_GROUP, POOL_GROUP), f32) * POOL_GROUP ** -0.5
    pool_scale = 1.0 + 0.02 * jax.random.normal(ks[7], (DEPTH, D_POOL), f32)
    w_kv = jax.random.normal(ks[8], (DEPTH, D_MODEL, 2 * D_ATT), f32) * D_MODEL ** -0.5
    w_out = jax.random.normal(ks[9], (DEPTH, D_MIX, D_MODEL), f32) * D_MIX ** -0.5
    post_norm_w = 1.0 + 0.02 * jax.random.normal(ks[10], (DEPTH, D_MODEL), f32)
    return {"x": x, "mem": mem, "pre_norm_w": pre_norm_w, "mem_norm_w": mem_norm_w,
            "w_in": w_in, "conv_w": conv_w, "pool_w": pool_w, "pool_scale": pool_scale,
            "w_kv": w_kv, "w_out": w_out, "post_norm_w": post_norm_w}


def reference(x, mem, pre_norm_w, mem_norm_w, w_in, conv_w, pool_w, pool_scale, w_kv, w_out, post_norm_w):
    for l in range(DEPTH):
        x = hybrid_layer(x, mem, pre_norm_w[l], mem_norm_w[l], w_in[l], conv_w[l], pool_w[l],
                         pool_scale[l], w_kv[l], w_out[l], post_norm_w[l])
    return x
```

```python
import numpy as np
from contextlib import ExitStack
import concourse.bass as bass
import concourse.mybir as mybir
from concourse.bass_utils import run_bass_kernel_spmd

F32 = mybir.dt.float32
BF16 = mybir.dt.bfloat16
I32 = mybir.dt.int32
AF = mybir.ActivationFunctionType
ALU = mybir.AluOpType

ENGS = ("pe", "act", "dve", "pool", "sp")


class Op:
    __slots__ = ("idx", "eng", "fn", "deps", "chan", "sig", "count", "sem", "name")


class Prog:
    def __init__(self, nc):
        self.nc = nc
        self.ops = []
        self.last_w = {}
        self.readers = {}

    def op(self, eng, fn, r=(), w=(), chan=None, name=None):
        o = Op()
        o.idx = len(self.ops)
        o.eng = eng
        o.fn = fn
        o.chan = chan
        o.sig = False
        o.count = None
        o.sem = None
        o.name = name
        deps = {}
        for k in r:
            p = self.last_w.get(k)
            if p is not None:
                deps[p] = "raw"
        for k in w:
            p = self.last_w.get(k)
            if p is not None and p not in deps:
                deps[p] = "waw"
            for rd in self.readers.get(k, ()):
                if rd not in deps:
                    deps[rd] = "war"
        o.deps = deps
        for k in r:
            self.readers.setdefault(k, []).append(o.idx)
        for k in w:
            self.last_w[k] = o.idx
            self.readers[k] = []
        self.ops.append(o)
        return o

    def _needs_wait(self, o, p, kind):
        if p.chan is not None:
            return True
        if p.eng != o.eng:
            return True
        if o.chan is not None:
            return True
        if o.eng == "pe":
            return False
        return True

    def emit(self, stack):
        nc = self.nc
        ops = self.ops
        for o in ops:
            for pi, kind in o.deps.items():
                p = ops[pi]
                if self._needs_wait(o, p, kind):
                    p.sig = True
        eng_sem = {e: stack.enter_context(nc.semaphore("s_" + e)) for e in ENGS}
        chan_sem = {}
        eng_cnt = {e: 0 for e in ENGS}
        chan_cnt = {}
        for o in ops:
            if o.chan is not None:
                if o.chan not in chan_sem:
                    chan_sem[o.chan] = stack.enter_context(nc.semaphore("c_" + o.chan))
                    chan_cnt[o.chan] = 0
                chan_cnt[o.chan] += 16
                o.sem = chan_sem[o.chan]
                o.count = chan_cnt[o.chan]
                o.sig = True
            elif o.sig:
                eng_cnt[o.eng] += 1
                o.sem = eng_sem[o.eng]
                o.count = eng_cnt[o.eng]
        self.n_sems = len(eng_sem) + len(chan_sem)
        block = stack.enter_context(nc.Block())

        def run(engname):
            def body(eng):
                waited = {}
                for o in ops:
                    if o.eng != engname:
                        continue
                    for pi, kind in o.deps.items():
                        p = ops[pi]
                        if not self._needs_wait(o, p, kind):
                            continue
                        key = id(p.sem)
                        if waited.get(key, 0) >= p.count:
                            continue
                        eng.wait_ge(p.sem, p.count)
                        waited[key] = p.count
                    ins = o.fn(eng)
                    if o.sig:
                        ins.then_inc(o.sem, 16 if o.chan is not None else 1)
                for o in ops:
                    if o.eng == engname and o.chan is not None:
                        key = id(o.sem)
                        if waited.get(key, 0) < o.count:
                            eng.wait_ge(o.sem, o.count)
                            waited[key] = o.count
            return body

        block.tensor(run("pe"))
        block.scalar(run("act"))
        block.vector(run("dve"))
        block.gpsimd(run("pool"))
        block.sync(run("sp"))


D = 1024
KB = 8
T = 512
NM = 256
DIN = 2816
EPS = 1e-6

def _colblocks():
    blocks = []
    for j in range(3):
        blocks.append(("xp%d" % j, [(1536 + 128 * j, 128)]))
    for j in range(3):
        blocks.append(("gc%d" % j, [(1152 + 128 * j, 128)]))
    for j in range(3):
        blocks.append(("gp%d" % j, [(1920 + 128 * j, 128)]))
    for i in range(2):
        blocks.append(("ga%d" % i, [(2560 + 128 * i, 128)]))
    for i in range(2):
        blocks.append(("q%d" % i, [(2304 + 128 * i, 128)]))
    for j in range(3):
        blocks.append(("xc%d" % j, [(0 + 128 * j, 128)]))
        blocks.append(("cg%d" % j, [(768 + 128 * j, 128)]))
        blocks.append(("bg%d" % j, [(384 + 128 * j, 128)]))
    return blocks


COLBLOCKS = _colblocks()
BLKPOS = {name: i for i, (name, _) in enumerate(COLBLOCKS)}
POOL_RANGES = [
    [(0, 96, 2), (96, 128, 4)],
    [(0, 64, 4), (64, 128, 8)],
    [(0, 32, 8), (32, 64, 16), (64, 128, 16)],
]


def _pool_pairs():
    pairs = set()
    for g in range(4):
        blks = sorted({r // 128 for r in range(96 * g, 96 * g + 96)})
        for a in blks:
            for b in blks:
                pairs.add((a, b))
    return sorted(pairs)


POOL_PAIRS = _pool_pairs()


def build_nc(S):
    NCH = S // T
    NT = S // 128
    nc = bass.Bass("TRN2", target_bir_lowering=False)
    x_d = nc.dram_tensor("x", [S, D], F32, kind="ExternalInput").ap()
    mem_d = nc.dram_tensor("mem", [NM, D], F32, kind="ExternalInput").ap()
    prew_d = nc.dram_tensor("pre_norm_w", [D], F32, kind="ExternalInput").ap()
    memw_d = nc.dram_tensor("mem_norm_w", [D], F32, kind="ExternalInput").ap()
    win_d = nc.dram_tensor("w_in", [D, DIN], F32, kind="ExternalInput").ap()
    convw_d = nc.dram_tensor("conv_w", [3, 384], F32, kind="ExternalInput").ap()
    poolw_d = nc.dram_tensor("pool_w", [4, 96, 96], F32, kind="ExternalInput").ap()
    pscale_d = nc.dram_tensor("pool_scale", [384], F32, kind="ExternalInput").ap()
    wkv_d = nc.dram_tensor("w_kv", [D, 512], F32, kind="ExternalInput").ap()
    wout_d = nc.dram_tensor("w_out", [D, D], F32, kind="ExternalInput").ap()
    postw_d = nc.dram_tensor("post_norm_w", [1, D], F32, kind="ExternalInput").ap()
    out_d = nc.dram_tensor("out", [S, D], F32, kind="ExternalOutput").ap()

    st = ExitStack()
    with st:
        def sb(name, shape, dt):
            return st.enter_context(nc.sbuf_tensor(name, shape, dt))

        w_in_sb = sb("w_in_sb", [128, KB, DIN], BF16)
        w_out_sb = sb("w_out_sb", [128, KB, D], BF16)
        wbig = sb("wbig", [128, 3, 384], BF16)
        kT = sb("kT", [128, 2, NM], BF16)
        v_sb = sb("v_sb", [128, 2, 256], BF16)
        ident_i = sb("ident_i", [128, 128], I32)
        ident_b = sb("ident_b", [128, 128], BF16)
        ones_b = sb("ones_b", [128, 64], BF16)
        eps_col = sb("eps_col", [128, 1], F32)
        prew_col = sb("prew_col", [128, KB], F32)
        memw_col = sb("memw_col", [128, KB], F32)
        convw_col = sb("convw_col", [128, 3, 3], F32)
        pscale_col = sb("pscale_col", [128, 3], F32)
        postw_bc = sb("postw_bc", [128, D], F32)
        wcol = sb("wcol", [128, 3], F32)
        iot_i = sb("iot_i", [128, 16], I32)
        iot_f = sb("iot_f", [128, 16], F32)
        cnt_f = sb("cnt_f", [128, 3, 16], F32)
        invcnt = sb("invcnt", [128, 3, 16], F32)
        fix_t = sb("fix_t", [128, 16], F32)
        dummy = sb("dummy_act", [128, 2], F32)

        NXIN, NHB = 4, 4
        xin = [sb("xin%d" % i, [128, D], F32) for i in range(NXIN)]
        hb = [sb("hb%d" % i, [128, D], BF16) for i in range(NHB + 2)]
        xT = [sb("xT%d" % i, [128, KB, T], BF16) for i in range(2)]
        YT = [sb("YT%d" % i, [128, KB, T], BF16) for i in range(2)]
        junks = [sb("junk%d" % i, [128, D], BF16) for i in range(2)]
        junk_ctr = [0]
        ss_in = sb("ss_in", [128, 4], F32)
        ln_in = sb("ln_in", [128, 4], F32)
        rstd_in = sb("rstd_in", [128, 4], F32)
        ss_y = sb("ss_y", [128, 4], F32)
        ln_y = sb("ln_y", [128, 4], F32)
        rstd_y = sb("rstd_y", [128, 4], F32)
        xcs = [sb("xcs%d" % i, [128, T], F32) for i in range(2)]
        U = [[sb("U%d_%d" % (j, p), [128, T + 2], F32) for p in range(1)] for j in range(3)]
        ta = [sb("ta%d" % i, [128, T], F32) for i in range(2)]
        sgc = [sb("sgc%d" % i, [128, T], F32) for i in range(3)]
        sgp = [sb("sgp%d" % i, [128, T], BF16) for i in range(3)]
        sga = [sb("sga%d" % i, [128, T], BF16) for i in range(2)]
        XP = [[sb("XP%d_%d" % (j, p), [128, T + 16], F32) for p in range(1)] for j in range(3)]
        sA = sb("sA", [128, T + 16], F32)
        sB = sb("sB", [128, T + 16], F32)
        diff_b = sb("diff_b", [128, 3, T], BF16)
        qb = sb("qb", [128, 2, T], BF16)
        E = sb("E", [128, 8, T], BF16)
        Ld = sb("Ld", [128, T], F32)
        R = [sb("R%d" % i, [128, T], F32) for i in range(2)]
        tpost = [sb("tpost%d" % i, [128, D], F32) for i in range(4)]
        ps_all = st.enter_context(nc.psum_tensor("ps_all", [128, 8, 512], F32))

        P = Prog(nc)
        bank_ctr = [0]

        def next_bank():
            b = bank_ctr[0] % 8
            bank_ctr[0] += 1
            return b

        def next_bank_pair():
            if bank_ctr[0] % 8 == 7:
                bank_ctr[0] += 1
            b = bank_ctr[0] % 8
            bank_ctr[0] += 2
            return b

        def pk(b):
            return ("ps", b)

        P.op("pool", lambda e: e.iota(ident_i[:], pattern=[[1, 128]], base=0, channel_multiplier=-1),
             w=["ident_i"])
        P.op("dve", lambda e: e.tensor_scalar(out=ident_b[:], in0=ident_i[:], scalar1=0.0, scalar2=None,
                                              op0=ALU.is_equal), r=["ident_i"], w=["ident_b"])
        P.op("dve", lambda e: e.memset(ones_b[:], 1.0), w=["ones_b"])
        P.op("dve", lambda e: e.memset(eps_col[:], EPS), w=["eps_col"])
        P.op("dve", lambda e: e.memset(wbig[:], 0.0), w=["wbig"])
        P.op("act", lambda e: e.activation(out=ln_in[:, 0:1], in_=eps_col[:], func=AF.Exp), r=["eps_col"], w=[("in", "ln", 0)])
        for j in range(3):
            for ri, (lo, hi, w_) in enumerate(POOL_RANGES[j]):
                P.op("dve", lambda e, j=j, lo=lo, hi=hi, w_=w_: e.memset(wcol[lo:hi, j:j + 1], float(w_)),
                     w=[("wcol", j, ri)])
            P.op("pool", lambda e, j=j: e.memset(XP[j][0][:, T:T + 16], 0.0), w=[("XPtail", j, 0)])
            P.op("pool", lambda e, j=j: e.memset(U[j][0][:, T:T + 2], 0.0), w=[("Utail", j, 0)])
        P.op("pool", lambda e: e.iota(iot_i[:], pattern=[[1, 16]], base=1, channel_multiplier=0), w=["iot_i"])
        P.op("dve", lambda e: e.tensor_copy(out=iot_f[:], in_=iot_i[:]), r=["iot_i"], w=["iot_f"])
        for j in range(3):
            P.op("dve", lambda e, j=j: e.tensor_scalar(out=cnt_f[:, j, :], in0=iot_f[:], scalar1=wcol[:, j:j + 1],
                                                       scalar2=None, op0=ALU.min),
                 r=["iot_f"] + [("wcol", j, ri) for ri in range(len(POOL_RANGES[j]))], w=[("cnt_f", j)])
            P.op("dve", lambda e, j=j: e.reciprocal(out=invcnt[:, j, :], in_=cnt_f[:, j, :]),
                 r=[("cnt_f", j)], w=[("invcnt", j)])

        def load_xin(t, extra_deps=()):
            slot = t % NXIN
            P.op("sp", lambda e, t=t, slot=slot: e.dma_start(out=xin[slot][:], in_=x_d[t * 128:(t + 1) * 128, :]),
                 r=list(extra_deps), w=[("xin", slot)], chan="xin%d" % slot)

        for t in range(4):
            load_xin(t)

        def small(e, out, in_):
            with nc.allow_non_contiguous_dma(reason="tiny parameter load"):
                return e.dma_start(out=out, in_=in_)

        P.op("sp", lambda e: small(e, prew_col[:], prew_d.rearrange("(kb p) -> p kb", p=128)),
             w=["prew_col"], chan="prew")
        P.op("sp", lambda e: small(e, memw_col[:], memw_d.rearrange("(kb p) -> p kb", p=128)),
             w=["memw_col"], chan="memw")
        for j in range(3):
            P.op("sp", lambda e, j=j: small(e, convw_col[:, j, :],
                                            convw_d[:, j * 128:(j + 1) * 128].rearrange("k p -> p k")),
                 w=[("convw", j)], chan="convw%d" % j)
        P.op("sp", lambda e: small(e, pscale_col[:], pscale_d.rearrange("(j p) -> p j", p=128)),
             w=["pscale"], chan="psc")
        P.op("sp", lambda e: e.dma_start(out=postw_bc[:], in_=postw_d.partition_broadcast(128)),
             w=["postw_bc"], chan="postw")

        win_v = win_d.rearrange("(kb p) c -> p kb c", p=128)
        wkv_v = wkv_d.rearrange("(kb p) c -> p kb c", p=128)
        wout_v = wout_d.rearrange("(kb p) c -> p kb c", p=128)

        def load_win(name):
            pos = BLKPOS[name] * 128
            off = 0
            for ri, (a, n) in enumerate(COLBLOCKS[BLKPOS[name]][1]):
                P.op("pool", lambda e, a=a, n=n, o=pos + off: e.dma_start(out=w_in_sb[:, :, o:o + n],
                                                                         in_=win_v[:, :, a:a + n]),
                     w=[("win", name, ri)], chan="win_%s_%d" % (name, ri))
                off += n

        def win_keys(name):
            return [("win", name, ri) for ri in range(len(COLBLOCKS[BLKPOS[name]][1]))]

        for i in range(2):
            P.op("pool", lambda e, i=i: e.dma_start(out=tpost[i][:], in_=mem_d[i * 128:(i + 1) * 128, :]),
                 w=[("tpost", i)], chan="memld%d" % i)
        for name, _ in COLBLOCKS[:3]:
            load_win(name)
        P.op("pool", lambda e: e.dma_start(out=E[:], in_=wkv_v), w=[("E", i) for i in range(8)], chan="wkv")
        wbig_keys = []
        for g in range(4):
            for ji in range(3):
                r0, r1 = max(96 * g, 128 * ji), min(96 * g + 96, 128 * ji + 128)
                if r0 >= r1:
                    continue
                P.op("pool", lambda e, g=g, ji=ji, r0=r0, r1=r1: e.dma_start(
                    out=wbig[r0 - 128 * ji:r1 - 128 * ji, ji, 96 * g:96 * g + 96],
                    in_=poolw_d[g, r0 - 96 * g:r1 - 96 * g, :]),
                    r=["wbig"], w=[("wbigd", g, ji)], chan="pw%d_%d" % (g, ji))
                wbig_keys.append(("wbigd", g, ji))
        for name, _ in COLBLOCKS[3:]:
            load_win(name)
        wout_rows = [[(128 * j, 128)] for j in range(3)]
        wout_rows += [[(384 + 128 * j, 128)] for j in range(3)]
        wout_rows += [[(768 + 128 * i, 128)] for i in range(2)]
        def load_wout(extra_deps=()):
            for kb in range(KB):
                off = 0
                for ri, (a, n) in enumerate(wout_rows[kb]):
                    P.op("pool", lambda e, kb=kb, a=a, n=n, off=off: e.dma_start(out=w_out_sb[off:off + n, kb, :],
                                                                                in_=wout_d[a:a + n, :]),
                         r=list(extra_deps), w=[("wout", kb, ri)], chan="wout%d_%d" % (kb, ri))
                    off += n

        def wout_keys(kb):
            return [("wout", kb, ri) for ri in range(len(wout_rows[kb]))]


        def sumsq(src_fn, src_keys, ssb, i, tag):
            jn = junk_ctr[0] % 2
            junk_ctr[0] += 1
            junk = junks[jn]
            P.op("act", lambda e: e.activation(out=(junk[:].rearrange("p (a f) -> p a f", a=2) if len(src_fn().shape) == 3 else junk[:]), in_=src_fn(), func=AF.Square, accum_out=ssb[:, i:i + 1]),
                 r=src_keys, w=[("junk", jn), (tag, "ss", i)])

        def rstd_batch(ssb, lnb, rsb, i0, i1, tag):
            P.op("act", lambda e: e.activation(out=lnb[:, i0:i1], in_=ssb[:, i0:i1], func=AF.Ln,
                                               bias=eps_col[:], scale=1.0 / D),
                 r=[(tag, "ss", i) for i in range(i0, i1)] + ["eps_col"], w=[(tag, "ln", i) for i in range(i0, i1)])
            P.op("act", lambda e: e.activation(out=rsb[:, i0:i1], in_=lnb[:, i0:i1], func=AF.Exp, scale=-0.5),
                 r=[(tag, "ln", i) for i in range(i0, i1)], w=[(tag, "rs", i) for i in range(i0, i1)])

        def transposes(hb_idx, ncols, scal_col, scal_key, dst, dst_key, evac="dve"):
            nt = len(hb_idx)
            bs = [next_bank() for _ in range(4)]
            bvs = [ps_all[:, b, :].bitcast(BF16) for b in bs]
            for ii, hi in enumerate(hb_idx):
                for kb in range(KB):
                    pr, kk = kb // 2, kb % 2
                    P.op("pe", lambda e, bv=bvs[pr], kk=kk, ii=ii, hi=hi, kb=kb: e.transpose(
                        out=bv[:, kk * 512 + ii * 128:kk * 512 + (ii + 1) * 128],
                        in_=hb[hi][:, kb * 128:(kb + 1) * 128], identity=ident_b[:]),
                        r=[("hb", hi), "ident_b"], w=[pk(bs[pr])])
            for kb in range(KB):
                pr, kk = kb // 2, kb % 2
                b = bs[pr]
                bv = bvs[pr]
                if evac == "dve":
                    P.op("dve", lambda e, bv=bv, kk=kk, kb=kb: e.tensor_scalar(
                        out=dst[:, kb, 0:nt * 128], in0=bv[:, kk * 512:kk * 512 + nt * 128],
                        scalar1=scal_col[:, kb:kb + 1], scalar2=None, op0=ALU.mult),
                        r=[pk(b), scal_key], w=[(dst_key, kb)])
                else:
                    P.op("act", lambda e, bv=bv, kk=kk, kb=kb: e.activation(
                        out=dst[:, kb, 0:nt * 128], in_=bv[:, kk * 512:kk * 512 + nt * 128], func=AF.Copy,
                        scale=scal_col[:, kb:kb + 1]),
                        r=[pk(b), scal_key], w=[(dst_key, kb)])

        def kv_prep_a():
            for i in range(2):
                sumsq(lambda i=i: tpost[i][:], [("tpost", i)], ss_y, i, "y")
            rstd_batch(ss_y, ln_y, rstd_y, 0, 2, "y")
            for i in range(2):
                P.op("dve", lambda e, i=i: e.tensor_scalar(out=hb[NHB + i][:], in0=tpost[i][:], scalar1=rstd_y[:, i:i + 1],
                                                           scalar2=None, op0=ALU.mult),
                     r=[("tpost", i), ("y", "rs", i)], w=[("hb", NHB + i)])

        def kv_prep_t():
            transposes([NHB, NHB + 1], 2, memw_col, "memw_col", YT[1], "YT1", evac="act")

        def kv_prep_b():
            memT = YT[1]
            for p in range(2):
                b = next_bank()
                for kb in range(KB):
                    P.op("pe", lambda e, b=b, p=p, kb=kb: e.matmul(ps_all[:, b, 0:NM], lhsT=E[:, kb, p * 128:(p + 1) * 128],
                                                                  rhs=memT[:, kb, 0:NM], start=(kb == 0), stop=(kb == KB - 1)),
                         r=[("E", kb), ("YT1", kb)], w=[pk(b)])
                P.op("act", lambda e, b=b, p=p: e.activation(out=kT[:, p, :], in_=ps_all[:, b, 0:NM], func=AF.Copy),
                     r=[pk(b)], w=[("kT", p)])
            for mb in range(2):
                b = next_bank()
                for kb in range(KB):
                    P.op("pe", lambda e, b=b, mb=mb, kb=kb: e.matmul(ps_all[:, b, 0:256], lhsT=memT[:, kb, mb * 128:(mb + 1) * 128],
                                                                    rhs=E[:, kb, 256:512], start=(kb == 0), stop=(kb == KB - 1)),
                         r=[("E", kb), ("YT1", kb)], w=[pk(b)])
                P.op("act", lambda e, b=b, mb=mb: e.activation(out=v_sb[:, mb, :], in_=ps_all[:, b, 0:256], func=AF.Copy),
                     r=[pk(b)], w=[("v", mb)])


        def input_stage(c):
            for i in range(4):
                t = 4 * c + i
                slot = t % NXIN
                sumsq(lambda slot=slot: xin[slot][:], [("xin", slot)], ss_in, i, "in")
            rstd_batch(ss_in, ln_in, rstd_in, 0, 4, "in")
            for i in range(4):
                t = 4 * c + i
                slot = t % NXIN
                P.op("dve", lambda e, slot=slot, i=i: e.tensor_scalar(out=hb[i][:], in0=xin[slot][:],
                                                                      scalar1=rstd_in[:, i:i + 1], scalar2=None,
                                                                      op0=ALU.mult),
                     r=[("xin", slot), ("in", "rs", i)], w=[("hb", i)])
                if t < NT - 4 and c > 0:
                    P.op("sp", lambda e, t=t, slot=slot: e.dma_start(out=out_d[t * 128:(t + 1) * 128, :], in_=xin[slot][:]),
                         r=[("xin", slot)], w=[("out_t", t)], chan="xst%d" % slot)
                if t + NXIN < NT:
                    load_xin(t + NXIN, extra_deps=(win_keys("q1") if c == 0 else ()))
            if c == 0 and NT > 4:
                for i in range(4):
                    P.op("sp", lambda e, i=i: e.dma_start(out=out_d[i * 128:(i + 1) * 128, :], in_=x_d[i * 128:(i + 1) * 128, :]),
                         r=win_keys(COLBLOCKS[-1][0]) + [("xin", s_) for s_ in range(NXIN)], w=[("out_t", i)],
                         chan="cp0_%d" % i)

        def transpose_stage(c):
            transposes([0, 1, 2, 3], 4, prew_col, "prew_col", xT[c % 2], "xT%d" % (c % 2))

        def inproj(c, name):
            b = next_bank()
            pos = BLKPOS[name] * 128
            xt = xT[c % 2]
            for kb in range(KB):
                P.op("pe", lambda e, b=b, pos=pos, kb=kb, xt=xt: e.matmul(
                    ps_all[:, b, :], lhsT=w_in_sb[:, kb, pos:pos + 128], rhs=xt[:, kb, :],
                    start=(kb == 0), stop=(kb == KB - 1)),
                    r=win_keys(name) + [("xT%d" % (c % 2), kb)], w=[pk(b)])
            return b

        def chunk_main(c):
            yt = YT[c % 2]
            ytk = "YT%d" % (c % 2)
            cur, prv = 0, 0
            for j in range(3):
                xp = XP[j][cur]
                xpp = XP[j][prv]
                P.op("act", lambda e, xp=xp, xpp=xpp: e.activation(out=xp[:, 0:16], in_=xpp[:, T:T + 16], func=AF.Copy),
                     r=[("XPtail", j, prv)], w=[("XPhist", j, cur)])
                u = U[j][cur]
                up = U[j][prv]
                P.op("act", lambda e, u=u, up=up: e.activation(out=u[:, 0:2], in_=up[:, T:T + 2], func=AF.Copy),
                     r=[("Utail", j, prv)], w=[("Uhist", j, cur)])
            for j in range(3):
                b = inproj(c, "xp%d" % j)
                xp = XP[j][cur]
                xpp = XP[j][prv]
                P.op("act", lambda e, b=b, xp=xp: e.activation(out=xp[:, 16:16 + T], in_=ps_all[:, b, :], func=AF.Copy),
                     r=[pk(b)], w=[("XPmain", j, cur), ("XPtail", j, cur)])
                xk = [("XPmain", j, cur), ("XPtail", j, cur), ("XPhist", j, cur)]
                W = T + 16
                ranges = POOL_RANGES[j]

                def legal(lo, hi):
                    out = []
                    while lo < hi:
                        if lo == 0:
                            nxt = hi
                        elif lo == 64:
                            nxt = min(hi, 128)
                        else:
                            nxt = min(hi, lo + 32)
                        out.append((lo, nxt))
                        lo = nxt
                    return out

                def qk(name, lo, hi):
                    return [(name, q) for q in range(lo // 32, hi // 32)]

                def finalize(src, sname, lo, hi, w, j=j, xp=xp, xk=xk):
                    P.op("dve", lambda e: e.scalar_tensor_tensor(out=diff_b[lo:hi, j, :], in0=src[lo:hi, 16:W],
                                                                 scalar=1.0 / w, in1=xp[lo:hi, 16:W],
                                                                 op0=ALU.mult, op1=ALU.subtract),
                         r=qk(sname, lo, hi) + xk, w=qk(("diff", j), lo, hi))
                    if c == 0:
                        P.op("dve", lambda e: e.tensor_tensor(out=fix_t[lo:hi, :], in0=src[lo:hi, 16:32],
                                                              in1=invcnt[lo:hi, j, :], op=ALU.mult),
                             r=qk(sname, lo, hi) + [("invcnt", j)], w=qk("fix_t", lo, hi))
                        P.op("dve", lambda e: e.tensor_tensor(out=diff_b[lo:hi, j, 0:16], in0=fix_t[lo:hi, :],
                                                              in1=xp[lo:hi, 16:32], op=ALU.subtract),
                             r=qk("fix_t", lo, hi) + xk, w=qk(("diff", j), lo, hi))

                P.op("dve", lambda e, xp=xp: e.tensor_tensor(out=sA[:, 1:W], in0=xp[:, 1:W], in1=xp[:, 0:W - 1], op=ALU.add),
                     r=xk, w=qk("sA", 0, 128))
                src, sname, dst, dname = sA, "sA", sB, "sB"
                lvl = 2
                while True:
                    for lo, hi, w_ in ranges:
                        if w_ == lvl:
                            finalize(src, sname, lo, hi, w_)
                    need = [(lo, hi) for lo, hi, w_ in ranges if w_ > lvl]
                    if not need:
                        break
                    nlo, nhi = min(r[0] for r in need), max(r[1] for r in need)
                    for lo, hi in legal(nlo, nhi):
                        P.op("dve", lambda e, src=src, dst=dst, lvl=lvl, lo=lo, hi=hi: e.tensor_tensor(
                            out=dst[lo:hi, 2 * lvl - 1:W], in0=src[lo:hi, 2 * lvl - 1:W], in1=src[lo:hi, lvl - 1:W - lvl],
                            op=ALU.add), r=qk(sname, lo, hi), w=qk(dname, lo, hi))
                    src, sname, dst, dname = dst, dname, src, sname
                    lvl *= 2
            if c == 0:
                kv_prep_t()
            for j in range(3):
                b = inproj(c, "gc%d" % j)
                P.op("act", lambda e, b=b, j=j: e.activation(out=sgc[j][:], in_=ps_all[:, b, :], func=AF.Silu),
                     r=[pk(b)], w=[("sgc", j)])
            if c == 0:
                kv_prep_b()
            for j in range(3):
                b = inproj(c, "gp%d" % j)
                P.op("act", lambda e, b=b, j=j: e.activation(out=sgp[j][:], in_=ps_all[:, b, :], func=AF.Silu),
                     r=[pk(b)], w=[("sgp", j)])
            for i in range(2):
                b = inproj(c, "ga%d" % i)
                P.op("act", lambda e, b=b, i=i: e.activation(out=sga[i][:], in_=ps_all[:, b, :], func=AF.Silu),
                     r=[pk(b)], w=[("sga", i)])
            P.op("act", lambda e: e.activation(out=dummy[:, 0:1], in_=eps_col[:], func=AF.Exp), r=["eps_col"], w=["dummy"])
            for i in range(2):
                b = inproj(c, "q%d" % i)
                P.op("act", lambda e, b=b, i=i: e.activation(out=qb[:, i, :], in_=ps_all[:, b, :], func=AF.Copy),
                     r=[pk(b)], w=[("qb", i)])
            scores(c)
            for j in range(3):
                if c == 0 and j == 1 and NCH > 1:
                    poolmix(0)
                    input_stage(1)
                u = U[j][cur]
                up = U[j][prv]
                xs = xcs[j % 2]
                tj = ta[j % 2]
                sg = sgc[j]
                b = inproj(c, "xc%d" % j)
                P.op("act", lambda e, b=b, xs=xs: e.activation(out=xs[:], in_=ps_all[:, b, :], func=AF.Copy),
                     r=[pk(b)], w=[("xcs", j % 2)])
                b = inproj(c, "cg%d" % j)
                P.op("dve", lambda e, b=b, xs=xs, u=u: e.tensor_tensor(out=u[:, 2:2 + T], in0=ps_all[:, b, :], in1=xs[:],
                                                                     op=ALU.mult),
                     r=[pk(b), ("xcs", j % 2)], w=[("Umain", j, cur), ("Utail", j, cur)])
                uk = [("Umain", j, cur), ("Utail", j, cur), ("Uhist", j, cur)]
                P.op("act", lambda e, u=u, tj=tj, j=j: e.activation(out=tj[:], in_=u[:, 2:2 + T], func=AF.Copy,
                                                                   scale=convw_col[:, j, 2:3]),
                     r=uk + [("convw", j)], w=[("ta", j % 2)])
                P.op("dve", lambda e, u=u, tj=tj, j=j: e.scalar_tensor_tensor(out=tj[:], in0=u[:, 1:1 + T],
                                                                            scalar=convw_col[:, j, 1:2], in1=tj[:],
                                                                            op0=ALU.mult, op1=ALU.add),
                     r=uk + [("convw", j), ("ta", j % 2)], w=[("ta", j % 2)])
                P.op("dve", lambda e, u=u, tj=tj, j=j: e.scalar_tensor_tensor(out=tj[:], in0=u[:, 0:T],
                                                                            scalar=convw_col[:, j, 0:1], in1=tj[:],
                                                                            op0=ALU.mult, op1=ALU.add),
                     r=uk + [("convw", j), ("ta", j % 2)], w=[("ta", j % 2)])
                b = inproj(c, "bg%d" % j)
                P.op("dve", lambda e, b=b, tj=tj: e.tensor_tensor(out=tj[:], in0=ps_all[:, b, :], in1=tj[:], op=ALU.mult),
                     r=[pk(b), ("ta", j % 2)], w=[("ta", j % 2)])
                P.op("pool", lambda e, tj=tj, sg=sg, j=j: e.tensor_tensor(out=yt[:, j, :], in0=tj[:], in1=sg[:], op=ALU.mult),
                     r=[("ta", j % 2), ("sgc", j)], w=[(ytk, j)])

        def scores(c):
            for p in range(2):
                for mb in range(2):
                    bs = [next_bank(), next_bank()]
                    for i in range(2):
                        b = bs[i]
                        P.op("pe", lambda e, b=b, p=p, mb=mb, i=i: e.matmul(
                            ps_all[:, b, :], lhsT=kT[i * 64:(i + 1) * 64, p, mb * 128:(mb + 1) * 128],
                            rhs=qb[i * 64:(i + 1) * 64, p, :], start=True, stop=True),
                            r=[("kT", p), ("qb", p)], w=[pk(b)])
                    for i in range(2):
                        b = bs[i]
                        ei = (p * 2 + i) * 2 + mb
                        P.op("act", lambda e, b=b, ei=ei: e.activation(out=E[:, ei, :], in_=ps_all[:, b, :], func=AF.Exp,
                                                                      scale=0.125),
                             r=[pk(b)], w=[("E", ei)])

        def poolmix(c):
            cur = c % 2
            yt = YT[cur]
            ytk = "YT%d" % cur
            for jo in range(3):
                b = next_bank()
                jis = [ji for (ji, jo_) in POOL_PAIRS if jo_ == jo]
                for n, ji in enumerate(jis):
                    P.op("pe", lambda e, b=b, jo=jo, ji=ji, n=n, last=len(jis) - 1: e.matmul(
                        ps_all[:, b, :], lhsT=wbig[:, ji, jo * 128:(jo + 1) * 128], rhs=diff_b[:, ji, :],
                        start=(n == 0), stop=(n == last)),
                        r=wbig_keys + ["wbig"] + [(("diff", ji), q) for q in range(4)], w=[pk(b)])
                P.op("dve", lambda e, b=b, jo=jo: e.scalar_tensor_tensor(
                    out=yt[:, 3 + jo, :], in0=ps_all[:, b, :], scalar=pscale_col[:, jo:jo + 1], in1=sgp[jo][:],
                    op0=ALU.mult, op1=ALU.mult),
                    r=[pk(b), "pscale", ("sgp", jo)], w=[(ytk, 3 + jo)])

        def attn_out(c):
            cur = c % 2
            yt = YT[cur]
            ytk = "YT%d" % cur
            for p in range(2):
                bo = next_bank()
                bd = next_bank()
                for i in range(2):
                    for mb in range(2):
                        ei = (p * 2 + i) * 2 + mb
                        h = p * 2 + i
                        P.op("pe", lambda e, bo=bo, i=i, mb=mb, ei=ei, h=h: e.matmul(
                            ps_all[i * 64:(i + 1) * 64, bo, :], lhsT=v_sb[:, mb, h * 64:(h + 1) * 64], rhs=E[:, ei, :],
                            start=(mb == 0), stop=(mb == 1)),
                            r=[("v", mb), ("E", ei)], w=[("psh", bo, i)] + ([pk(bo)] if (i == 0 and mb == 0) else []))
                for i in range(2):
                    for mb in range(2):
                        ei = (p * 2 + i) * 2 + mb
                        P.op("pe", lambda e, bd=bd, i=i, mb=mb, ei=ei: e.matmul(
                            ps_all[i * 64:(i + 1) * 64, bd, :], lhsT=ones_b[:, 0:64], rhs=E[:, ei, :],
                            start=(mb == 0), stop=(mb == 1)),
                            r=["ones_b", ("E", ei)], w=[("psh", bd, i)] + ([pk(bd)] if (i == 0 and mb == 0) else []))
                r_ = R[p]
                P.op("act", lambda e, bd=bd: e.activation(out=Ld[:], in_=ps_all[:, bd, :], func=AF.Ln),
                     r=[pk(bd), ("psh", bd, 0), ("psh", bd, 1)], w=["Ld"])
                P.op("act", lambda e, r_=r_: e.activation(out=r_[:], in_=Ld[:], func=AF.Exp, scale=-1.0),
                     r=["Ld"], w=[("R", p)])
                P.op("dve", lambda e, bo=bo, r_=r_: e.tensor_tensor(out=r_[:], in0=ps_all[:, bo, :], in1=r_[:], op=ALU.mult),
                     r=[pk(bo), ("psh", bo, 0), ("psh", bo, 1), ("R", p)], w=[("R", p)])
                P.op("pool", lambda e, r_=r_, p=p: e.tensor_tensor(out=yt[:, 6 + p, :], in0=r_[:], in1=sga[p][:], op=ALU.mult),
                     r=[("R", p), ("sga", p)], w=[(ytk, 6 + p)])

        def post_tiles(c, tiles):
            cur = c % 2
            yt = YT[cur]
            ytk = "YT%d" % cur
            banks = {}
            for i in tiles:
                b = next_bank_pair()
                banks[i] = b
                for half in range(2):
                    for kb in range(KB):
                        P.op("pe", lambda e, b=b, half=half, kb=kb, i=i: e.matmul(
                            ps_all[:, b + half, :], lhsT=yt[:, kb, i * 128:(i + 1) * 128],
                            rhs=w_out_sb[:, kb, half * 512:(half + 1) * 512], start=(kb == 0), stop=(kb == KB - 1)),
                            r=wout_keys(kb) + [(ytk, kb)], w=[pk(b + half)])
                yv = ps_all[:, b:b + 2, :]
                sumsq(lambda yv=yv: yv, [pk(b), pk(b + 1)], ss_y, i, "y")
            rstd_batch(ss_y, ln_y, rstd_y, tiles[0], tiles[-1] + 1, "y")
            for i in tiles:
                t = 4 * c + i
                slot = t % 4
                b = banks[i]
                yv = ps_all[:, b:b + 2, :]
                P.op("dve", lambda e, yv=yv, i=i, slot=slot: e.scalar_tensor_tensor(
                    out=tpost[slot][:].rearrange("p (a f) -> p a f", a=2), in0=yv, scalar=rstd_y[:, i:i + 1],
                    in1=postw_bc[:].rearrange("p (a f) -> p a f", a=2), op0=ALU.mult, op1=ALU.mult),
                    r=[pk(b), pk(b + 1), ("y", "rs", i), "postw_bc"], w=[("tpost", slot)])
                if t >= NT - 4:
                    xs_ = t % NXIN
                    P.op("dve", lambda e, slot=slot, xs_=xs_: e.tensor_tensor(out=tpost[slot][:], in0=tpost[slot][:],
                                                                             in1=xin[xs_][:], op=ALU.add),
                         r=[("tpost", slot), ("xin", xs_)], w=[("tpost", slot)])
                    P.op("sp", lambda e, t=t, slot=slot: e.dma_start(out=out_d[t * 128:(t + 1) * 128, :], in_=tpost[slot][:]),
                         r=[("tpost", slot)], chan="fin%d" % slot)
                else:
                    P.op("pool", lambda e, t=t, slot=slot: e.dma_start(out=out_d[t * 128:(t + 1) * 128, :],
                                                                      in_=tpost[slot][:], accum_op=ALU.add),
                         r=[("tpost", slot), ("out_t", t)], chan="acc%d" % slot)


        def post(c, tiles):
            for i in tiles:
                post_tiles(c, [i])

        input_stage(0)
        transpose_stage(0)
        kv_prep_a()
        load_wout(extra_deps=[("xin", s_) for s_ in range(NXIN)] if NT > 4 else ())
        for c in range(NCH + 1):
            if c < NCH:
                chunk_main(c)
                if c == 0 and NCH > 1:
                    transpose_stage(1)
                if not (c == 0 and NCH > 1):
                    poolmix(c)
                if c + 1 < NCH and c > 0:
                    input_stage(c + 1)
                attn_out(c)
            if c >= 1:
                post(c - 1, [0, 1])
            if c + 1 < NCH and c > 0:
                transpose_stage(c + 1)
            if c >= 1:
                post(c - 1, [2, 3])
            if c + 1 < NCH:
                P.op("act", lambda e: e.activation(out=dummy[:, 1:2], in_=eps_col[:], func=AF.Silu), r=["eps_col"], w=["dummy"])
        P.emit(st)
    return nc


_NC_CACHE = {}


def kernel(x, mem, pre_norm_w, mem_norm_w, w_in, conv_w, pool_w, pool_scale, w_kv, w_out, post_norm_w):
    x = np.asarray(x, dtype=np.float32)
    B, S, _ = x.shape
    if S not in _NC_CACHE:
        _NC_CACHE[S] = build_nc(S)
    nc = _NC_CACHE[S]
    f = lambda a: np.ascontiguousarray(np.asarray(a, dtype=np.float32))
    shared = {
        "pre_norm_w": f(pre_norm_w).reshape(D),
        "mem_norm_w": f(mem_norm_w).reshape(D),
        "w_in": f(w_in).reshape(D, DIN),
        "conv_w": f(conv_w).reshape(3, 384),
        "pool_w": f(pool_w).reshape(4, 96, 96),
        "pool_scale": f(pool_scale).reshape(384),
        "w_kv": f(w_kv).reshape(D, 512),
        "w_out": f(w_out).reshape(D, D),
        "post_norm_w": f(post_norm_w).reshape(1, D),
    }
    mem = np.asarray(mem, dtype=np.float32)
    in_maps = []
    for b in range(B):
        m = dict(shared)
        m["x"] = np.ascontiguousarray(x[b])
        m["mem"] = np.ascontiguousarray(mem[b])
        in_maps.append(m)
    res = run_bass_kernel_spmd(nc, in_maps, core_ids=list(range(B)))
    return np.stack([np.asarray(r["out"]) for r in res.results], axis=0).astype(np.float32)
```

```python
import numpy as np
from contextlib import ExitStack
import concourse.bass as bass
import concourse.mybir as mybir
from concourse.bass_utils import run_bass_kernel_spmd

F32 = mybir.dt.float32
BF16 = mybir.dt.bfloat16
I32 = mybir.dt.int32
AF = mybir.ActivationFunctionType
ALU = mybir.AluOpType

ENGS = ("pe", "act", "dve", "pool", "sp")


class Op:
    __slots__ = ("idx", "eng", "fn", "deps", "chan", "sig", "count", "sem", "name")


class Prog:
    def __init__(self, nc):
        self.nc = nc
        self.ops = []
        self.last_w = {}
        self.readers = {}

    def op(self, eng, fn, r=(), w=(), chan=None, name=None):
        o = Op()
        o.idx = len(self.ops)
        o.eng = eng
        o.fn = fn
        o.chan = chan
        o.sig = False
        o.count = None
        o.sem = None
        o.name = name
        deps = {}
        for k in r:
            p = self.last_w.get(k)
            if p is not None:
                deps[p] = "raw"
        for k in w:
            p = self.last_w.get(k)
            if p is not None and p not in deps:
                deps[p] = "waw"
            for rd in self.readers.get(k, ()):
                if rd not in deps:
                    deps[rd] = "war"
        o.deps = deps
        for k in r:
            self.readers.setdefault(k, []).append(o.idx)
        for k in w:
            self.last_w[k] = o.idx
            self.readers[k] = []
        self.ops.append(o)
        return o

    def _needs_wait(self, o, p, kind):
        if p.chan is not None:
            return True
        if p.eng != o.eng:
            return True
        if o.chan is not None:
            return True
        if o.eng == "pe":
            return False
        return True

    def emit(self, stack):
        nc = self.nc
        ops = self.ops
        for o in ops:
            for pi, kind in o.deps.items():
                p = ops[pi]
                if self._needs_wait(o, p, kind):
                    p.sig = True
        eng_sem = {e: stack.enter_context(nc.semaphore("s_" + e)) for e in ENGS}
        chan_sem = {}
        eng_cnt = {e: 0 for e in ENGS}
        chan_cnt = {}
        for o in ops:
            if o.chan is not None:
                if o.chan not in chan_sem:
                    chan_sem[o.chan] = stack.enter_context(nc.semaphore("c_" + o.chan))
                    chan_cnt[o.chan] = 0
                chan_cnt[o.chan] += 16
                o.sem = chan_sem[o.chan]
                o.count = chan_cnt[o.chan]
                o.sig = True
            elif o.sig:
                eng_cnt[o.eng] += 1
                o.sem = eng_sem[o.eng]
                o.count = eng_cnt[o.eng]
        self.n_sems = len(eng_sem) + len(chan_sem)
        block = stack.enter_context(nc.Block())

        def run(engname):
            def body(eng):
                waited = {}
                for o in ops:
                    if o.eng != engname:
                        continue
                    for pi, kind in o.deps.items():
                        p = ops[pi]
                        if not self._needs_wait(o, p, kind):
                            continue
                        key = id(p.sem)
                        if waited.get(key, 0) >= p.count:
                            continue
                        eng.wait_ge(p.sem, p.count)
                        waited[key] = p.count
                    ins = o.fn(eng)
                    if o.sig:
                        ins.then_inc(o.sem, 16 if o.chan is not None else 1)
                for o in ops:
                    if o.eng == engname and o.chan is not None:
                        key = id(o.sem)
                        if waited.get(key, 0) < o.count:
                            eng.wait_ge(o.sem, o.count)
                            waited[key] = o.count
            return body

        block.tensor(run("pe"))
        block.scalar(run("act"))
        block.vector(run("dve"))
        block.gpsimd(run("pool"))
        block.sync(run("sp"))


D = 1024
KB = 8
T = 512
NM = 256
DIN = 2816
EPS = 1e-6

def _colblocks():
    blocks = []
    for j in range(3):
        blocks.append(("xp%d" % j, [(1536 + 128 * j, 128)]))
    for j in range(3):
        blocks.append(("gc%d" % j, [(1152 + 128 * j, 128)]))
    for j in range(3):
        blocks.append(("gp%d" % j, [(1920 + 128 * j, 128)]))
    for i in range(2):
        blocks.append(("ga%d" % i, [(2560 + 128 * i, 128)]))
    for i in range(2):
        blocks.append(("q%d" % i, [(2304 + 128 * i, 128)]))
    for j in range(3):
        blocks.append(("xc%d" % j, [(0 + 128 * j, 128)]))
        blocks.append(("cg%d" % j, [(768 + 128 * j, 128)]))
        blocks.append(("bg%d" % j, [(384 + 128 * j, 128)]))
    return blocks


COLBLOCKS = _colblocks()
BLKPOS = {name: i for i, (name, _) in enumerate(COLBLOCKS)}
POOL_RANGES = [
    [(0, 96, 2), (96, 128, 4)],
    [(0, 64, 4), (64, 128, 8)],
    [(0, 32, 8), (32, 64, 16), (64, 128, 16)],
]


def _pool_pairs():
    pairs = set()
    for g in range(4):
        blks = sorted({r // 128 for r in range(96 * g, 96 * g + 96)})
        for a in blks:
            for b in blks:
                pairs.add((a, b))
    return sorted(pairs)


POOL_PAIRS = _pool_pairs()


def build_nc(S):
    NCH = S // T
    NT = S // 128
    nc = bass.Bass("TRN2", target_bir_lowering=False)
    x_d = nc.dram_tensor("x", [S, D], F32, kind="ExternalInput").ap()
    mem_d = nc.dram_tensor("mem", [NM, D], F32, kind="ExternalInput").ap()
    prew_d = nc.dram_tensor("pre_norm_w", [D], F32, kind="ExternalInput").ap()
    memw_d = nc.dram_tensor("mem_norm_w", [D], F32, kind="ExternalInput").ap()
    win_d = nc.dram_tensor("w_in", [D, DIN], F32, kind="ExternalInput").ap()
    convw_d = nc.dram_tensor("conv_w", [3, 384], F32, kind="ExternalInput").ap()
    poolw_d = nc.dram_tensor("pool_w", [4, 96, 96], F32, kind="ExternalInput").ap()
    pscale_d = nc.dram_tensor("pool_scale", [384], F32, kind="ExternalInput").ap()
    wkv_d = nc.dram_tensor("w_kv", [D, 512], F32, kind="ExternalInput").ap()
    wout_d = nc.dram_tensor("w_out", [D, D], F32, kind="ExternalInput").ap()
    postw_d = nc.dram_tensor("post_norm_w", [1, D], F32, kind="ExternalInput").ap()
    out_d = nc.dram_tensor("out", [S, D], F32, kind="ExternalOutput").ap()

    st = ExitStack()
    with st:
        def sb(name, shape, dt):
            return st.enter_context(nc.sbuf_tensor(name, shape, dt))

        w_in_sb = sb("w_in_sb", [128, KB, DIN], BF16)
        w_out_sb = sb("w_out_sb", [128, KB, D], BF16)
        wbig = sb("wbig", [128, 3, 384], BF16)
        kT = sb("kT", [128, 2, NM], BF16)
        v_sb = sb("v_sb", [128, 2, 256], BF16)
        ident_i = sb("ident_i", [128, 128], I32)
        ident_b = sb("ident_b", [128, 128], BF16)
        ones_b = sb("ones_b", [128, 64], BF16)
        eps_col = sb("eps_col", [128, 1], F32)
        prew_col = sb("prew_col", [128, KB], F32)
        memw_col = sb("memw_col", [128, KB], F32)
        convw_col = sb("convw_col", [128, 3, 3], F32)
        pscale_col = sb("pscale_col", [128, 3], F32)
        postw_bc = sb("postw_bc", [128, D], F32)
        wcol = sb("wcol", [128, 3], F32)
        iot_i = sb("iot_i", [128, 16], I32)
        iot_f = sb("iot_f", [128, 16], F32)
        cnt_f = sb("cnt_f", [128, 3, 16], F32)
        invcnt = sb("invcnt", [128, 3, 16], F32)
        fix_t = sb("fix_t", [128, 16], F32)
        dummy = sb("dummy_act", [128, 2], F32)

        NXIN, NHB = 4, 4
        xin = [sb("xin%d" % i, [128, D], F32) for i in range(NXIN)]
        hb = [sb("hb%d" % i, [128, D], BF16) for i in range(NHB + 2)]
        xT = [sb("xT%d" % i, [128, KB, T], BF16) for i in range(2)]
        YT = [sb("YT%d" % i, [128, KB, T], BF16) for i in range(2)]
        junks = [sb("junk%d" % i, [128, D], BF16) for i in range(2)]
        junk_ctr = [0]
        ss_in = sb("ss_in", [128, 4], F32)
        ln_in = sb("ln_in", [128, 4], F32)
        rstd_in = sb("rstd_in", [128, 4], F32)
        ss_y = sb("ss_y", [128, 4], F32)
        ln_y = sb("ln_y", [128, 4], F32)
        rstd_y = sb("rstd_y", [128, 4], F32)
        xcs = [sb("xcs%d" % i, [128, T], F32) for i in range(2)]
        U = [[sb("U%d_%d" % (j, p), [128, T + 2], F32) for p in range(1)] for j in range(3)]
        ta = [sb("ta%d" % i, [128, T], F32) for i in range(2)]
        sgc = [sb("sgc%d" % i, [128, T], F32) for i in range(3)]
        sgp = [sb("sgp%d" % i, [128, T], BF16) for i in range(3)]
        sga = [sb("sga%d" % i, [128, T], BF16) for i in range(2)]
        XP = [[sb("XP%d_%d" % (j, p), [128, T + 16], F32) for p in range(1)] for j in range(3)]
        sA = sb("sA", [128, T + 16], F32)
        sB = sb("sB", [128, T + 16], F32)
        diff_b = sb("diff_b", [128, 3, T], BF16)
        qb = sb("qb", [128, 2, T], BF16)
        E = sb("E", [128, 8, T], BF16)
        Ld = sb("Ld", [128, T], F32)
        R = [sb("R%d" % i, [128, T], F32) for i in range(2)]
        tpost = [sb("tpost%d" % i, [128, D], F32) for i in range(4)]
        ps_all = st.enter_context(nc.psum_tensor("ps_all", [128, 8, 512], F32))

        P = Prog(nc)
        bank_ctr = [0]

        def next_bank():
            b = bank_ctr[0] % 8
            bank_ctr[0] += 1
            return b

        def next_bank_pair():
            if bank_ctr[0] % 8 == 7:
                bank_ctr[0] += 1
            b = bank_ctr[0] % 8
            bank_ctr[0] += 2
            return b

        def pk(b):
            return ("ps", b)

        P.op("pool", lambda e: e.iota(ident_i[:], pattern=[[1, 128]], base=0, channel_multiplier=-1),
             w=["ident_i"])
        P.op("dve", lambda e: e.tensor_scalar(out=ident_b[:], in0=ident_i[:], scalar1=0.0, scalar2=None,
                                              op0=ALU.is_equal), r=["ident_i"], w=["ident_b"])
        P.op("dve", lambda e: e.memset(ones_b[:], 1.0), w=["ones_b"])
        P.op("dve", lambda e: e.memset(eps_col[:], EPS), w=["eps_col"])
        P.op("dve", lambda e: e.memset(wbig[:], 0.0), w=["wbig"])
        P.op("act", lambda e: e.activation(out=ln_in[:, 0:1], in_=eps_col[:], func=AF.Exp), r=["eps_col"], w=[("in", "ln", 0)])
        for j in range(3):
            for ri, (lo, hi, w_) in enumerate(POOL_RANGES[j]):
                P.op("dve", lambda e, j=j, lo=lo, hi=hi, w_=w_: e.memset(wcol[lo:hi, j:j + 1], float(w_)),
                     w=[("wcol", j, ri)])
            P.op("pool", lambda e, j=j: e.memset(XP[j][0][:, T:T + 16], 0.0), w=[("XPtail", j, 0)])
            P.op("pool", lambda e, j=j: e.memset(U[j][0][:, T:T + 2], 0.0), w=[("Utail", j, 0)])
        P.op("pool", lambda e: e.iota(iot_i[:], pattern=[[1, 16]], base=1, channel_multiplier=0), w=["iot_i"])
        P.op("dve", lambda e: e.tensor_copy(out=iot_f[:], in_=iot_i[:]), r=["iot_i"], w=["iot_f"])
        for j in range(3):
            P.op("dve", lambda e, j=j: e.tensor_scalar(out=cnt_f[:, j, :], in0=iot_f[:], scalar1=wcol[:, j:j + 1],
                                                       scalar2=None, op0=ALU.min),
                 r=["iot_f"] + [("wcol", j, ri) for ri in range(len(POOL_RANGES[j]))], w=[("cnt_f", j)])
            P.op("dve", lambda e, j=j: e.reciprocal(out=invcnt[:, j, :], in_=cnt_f[:, j, :]),
                 r=[("cnt_f", j)], w=[("invcnt", j)])

        def load_xin(t, extra_deps=()):
            slot = t % NXIN
            P.op("sp", lambda e, t=t, slot=slot: e.dma_start(out=xin[slot][:], in_=x_d[t * 128:(t + 1) * 128, :]),
                 r=list(extra_deps), w=[("xin", slot)], chan="xin%d" % slot)

        for t in range(4):
            load_xin(t)

        def small(e, out, in_):
            with nc.allow_non_contiguous_dma(reason="tiny parameter load"):
                return e.dma_start(out=out, in_=in_)

        P.op("sp", lambda e: small(e, prew_col[:], prew_d.rearrange("(kb p) -> p kb", p=128)),
             w=["prew_col"], chan="prew")
        P.op("sp", lambda e: small(e, memw_col[:], memw_d.rearrange("(kb p) -> p kb", p=128)),
             w=["memw_col"], chan="memw")
        for j in range(3):
            P.op("sp", lambda e, j=j: small(e, convw_col[:, j, :],
                                            convw_d[:, j * 128:(j + 1) * 128].rearrange("k p -> p k")),
                 w=[("convw", j)], chan="convw%d" % j)
        P.op("sp", lambda e: small(e, pscale_col[:], pscale_d.rearrange("(j p) -> p j", p=128)),
             w=["pscale"], chan="psc")
        P.op("sp", lambda e: e.dma_start(out=postw_bc[:], in_=postw_d.partition_broadcast(128)),
             w=["postw_bc"], chan="postw")

        win_v = win_d.rearrange("(kb p) c -> p kb c", p=128)
        wkv_v = wkv_d.rearrange("(kb p) c -> p kb c", p=128)
        wout_v = wout_d.rearrange("(kb p) c -> p kb c", p=128)

        def load_win(name):
            pos = BLKPOS[name] * 128
            off = 0
            for ri, (a, n) in enumerate(COLBLOCKS[BLKPOS[name]][1]):
                P.op("pool", lambda e, a=a, n=n, o=pos + off: e.dma_start(out=w_in_sb[:, :, o:o + n],
                                                                         in_=win_v[:, :, a:a + n]),
                     w=[("win", name, ri)], chan="win_%s_%d" % (name, ri))
                off += n

        def win_keys(name):
            return [("win", name, ri) for ri in range(len(COLBLOCKS[BLKPOS[name]][1]))]

        for i in range(2):
            P.op("pool", lambda e, i=i: e.dma_start(out=tpost[i][:], in_=mem_d[i * 128:(i + 1) * 128, :]),
                 w=[("tpost", i)], chan="memld%d" % i)
        P.op("pool", lambda e: e.dma_start(out=E[:], in_=wkv_v), w=[("E", i) for i in range(8)], chan="wkv")
        for name, _ in COLBLOCKS[:3]:
            load_win(name)
        wbig_keys = []
        for g in range(4):
            for ji in range(3):
                r0, r1 = max(96 * g, 128 * ji), min(96 * g + 96, 128 * ji + 128)
                if r0 >= r1:
                    continue
                P.op("pool", lambda e, g=g, ji=ji, r0=r0, r1=r1: e.dma_start(
                    out=wbig[r0 - 128 * ji:r1 - 128 * ji, ji, 96 * g:96 * g + 96],
                    in_=poolw_d[g, r0 - 96 * g:r1 - 96 * g, :]),
                    r=["wbig"], w=[("wbigd", g, ji)], chan="pw%d_%d" % (g, ji))
                wbig_keys.append(("wbigd", g, ji))
        for name, _ in COLBLOCKS[3:]:
            load_win(name)
        wout_rows = [[(128 * j, 128)] for j in range(3)]
        wout_rows += [[(384 + 128 * j, 128)] for j in range(3)]
        wout_rows += [[(768 + 128 * i, 128)] for i in range(2)]
        def load_wout(extra_deps=()):
            for kb in range(KB):
                off = 0
                for ri, (a, n) in enumerate(wout_rows[kb]):
                    P.op("pool", lambda e, kb=kb, a=a, n=n, off=off: e.dma_start(out=w_out_sb[off:off + n, kb, :],
                                                                                in_=wout_d[a:a + n, :]),
                         r=list(extra_deps), w=[("wout", kb, ri)], chan="wout%d_%d" % (kb, ri))
                    off += n

        def wout_keys(kb):
            return [("wout", kb, ri) for ri in range(len(wout_rows[kb]))]


        def sumsq(src_fn, src_keys, ssb, i, tag):
            jn = junk_ctr[0] % 2
            junk_ctr[0] += 1
            junk = junks[jn]
            P.op("act", lambda e: e.activation(out=(junk[:].rearrange("p (a f) -> p a f", a=2) if len(src_fn().shape) == 3 else junk[:]), in_=src_fn(), func=AF.Square, accum_out=ssb[:, i:i + 1]),
                 r=src_keys, w=[("junk", jn), (tag, "ss", i)])

        def rstd_batch(ssb, lnb, rsb, i0, i1, tag):
            P.op("act", lambda e: e.activation(out=lnb[:, i0:i1], in_=ssb[:, i0:i1], func=AF.Ln,
                                               bias=eps_col[:], scale=1.0 / D),
                 r=[(tag, "ss", i) for i in range(i0, i1)] + ["eps_col"], w=[(tag, "ln", i) for i in range(i0, i1)])
            P.op("act", lambda e: e.activation(out=rsb[:, i0:i1], in_=lnb[:, i0:i1], func=AF.Exp, scale=-0.5),
                 r=[(tag, "ln", i) for i in range(i0, i1)], w=[(tag, "rs", i) for i in range(i0, i1)])

        def transposes(hb_idx, ncols, scal_col, scal_key, dst, dst_key, evac="dve"):
            nt = len(hb_idx)
            bs = [next_bank() for _ in range(4)]
            bvs = [ps_all[:, b, :].bitcast(BF16) for b in bs]
            for ii, hi in enumerate(hb_idx):
                for kb in range(KB):
                    pr, kk = kb // 2, kb % 2
                    P.op("pe", lambda e, bv=bvs[pr], kk=kk, ii=ii, hi=hi, kb=kb: e.transpose(
                        out=bv[:, kk * 512 + ii * 128:kk * 512 + (ii + 1) * 128],
                        in_=hb[hi][:, kb * 128:(kb + 1) * 128], identity=ident_b[:]),
                        r=[("hb", hi), "ident_b"], w=[pk(bs[pr])])
            for kb in range(KB):
                pr, kk = kb // 2, kb % 2
                b = bs[pr]
                bv = bvs[pr]
                if evac == "dve" or (evac == "split" and pr < 2):
                    P.op("dve", lambda e, bv=bv, kk=kk, kb=kb: e.tensor_scalar(
                        out=dst[:, kb, 0:nt * 128], in0=bv[:, kk * 512:kk * 512 + nt * 128],
                        scalar1=scal_col[:, kb:kb + 1], scalar2=None, op0=ALU.mult),
                        r=[pk(b), scal_key], w=[(dst_key, kb)])
                else:
                    P.op("act", lambda e, bv=bv, kk=kk, kb=kb: e.activation(
                        out=dst[:, kb, 0:nt * 128], in_=bv[:, kk * 512:kk * 512 + nt * 128], func=AF.Copy,
                        scale=scal_col[:, kb:kb + 1]),
                        r=[pk(b), scal_key], w=[(dst_key, kb)])

        def kv_prep_a():
            for i in range(2):
                sumsq(lambda i=i: tpost[i][:], [("tpost", i)], ss_y, i, "y")
            rstd_batch(ss_y, ln_y, rstd_y, 0, 2, "y")
            for i in range(2):
                P.op("dve", lambda e, i=i: e.tensor_scalar(out=hb[NHB + i][:], in0=tpost[i][:], scalar1=rstd_y[:, i:i + 1],
                                                           scalar2=None, op0=ALU.mult),
                     r=[("tpost", i), ("y", "rs", i)], w=[("hb", NHB + i)])

        def kv_prep_t():
            transposes([NHB, NHB + 1], 2, memw_col, "memw_col", YT[1], "YT1", evac="act")

        def kv_prep_b():
            memT = YT[1]
            for p in range(2):
                b = next_bank()
                for kb in range(KB):
                    P.op("pe", lambda e, b=b, p=p, kb=kb: e.matmul(ps_all[:, b, 0:NM], lhsT=E[:, kb, p * 128:(p + 1) * 128],
                                                                  rhs=memT[:, kb, 0:NM], start=(kb == 0), stop=(kb == KB - 1)),
                         r=[("E", kb), ("YT1", kb)], w=[pk(b)])
                P.op("act", lambda e, b=b, p=p: e.activation(out=kT[:, p, :], in_=ps_all[:, b, 0:NM], func=AF.Copy),
                     r=[pk(b)], w=[("kT", p)])
            for mb in range(2):
                b = next_bank()
                for kb in range(KB):
                    P.op("pe", lambda e, b=b, mb=mb, kb=kb: e.matmul(ps_all[:, b, 0:256], lhsT=memT[:, kb, mb * 128:(mb + 1) * 128],
                                                                    rhs=E[:, kb, 256:512], start=(kb == 0), stop=(kb == KB - 1)),
                         r=[("E", kb), ("YT1", kb)], w=[pk(b)])
                P.op("act", lambda e, b=b, mb=mb: e.activation(out=v_sb[:, mb, :], in_=ps_all[:, b, 0:256], func=AF.Copy),
                     r=[pk(b)], w=[("v", mb)])


        def input_stage(c):
            for i in range(4):
                t = 4 * c + i
                slot = t % NXIN
                sumsq(lambda slot=slot: xin[slot][:], [("xin", slot)], ss_in, i, "in")
                if c == 0:
                    rstd_batch(ss_in, ln_in, rstd_in, i, i + 1, "in")
            if c > 0:
                rstd_batch(ss_in, ln_in, rstd_in, 0, 4, "in")
            for i in range(4):
                t = 4 * c + i
                slot = t % NXIN
                P.op("dve", lambda e, slot=slot, i=i: e.tensor_scalar(out=hb[i][:], in0=xin[slot][:],
                                                                      scalar1=rstd_in[:, i:i + 1], scalar2=None,
                                                                      op0=ALU.mult),
                     r=[("xin", slot), ("in", "rs", i)], w=[("hb", i)])
                if t < NT - 4 and c > 0:
                    P.op("sp", lambda e, t=t, slot=slot: e.dma_start(out=out_d[t * 128:(t + 1) * 128, :], in_=xin[slot][:]),
                         r=[("xin", slot)], w=[("out_t", t)], chan="xst%d" % slot)
                if t + NXIN < NT:
                    load_xin(t + NXIN, extra_deps=(win_keys("q1") if c == 0 else ()))
            if c == 0 and NT > 4:
                for i in range(4):
                    P.op("sp", lambda e, i=i: e.dma_start(out=out_d[i * 128:(i + 1) * 128, :], in_=x_d[i * 128:(i + 1) * 128, :]),
                         r=win_keys(COLBLOCKS[-1][0]) + [("xin", s_) for s_ in range(NXIN)], w=[("out_t", i)],
                         chan="cp0_%d" % i)

        def transpose_stage(c):
            transposes([0, 1, 2, 3], 4, prew_col, "prew_col", xT[c % 2], "xT%d" % (c % 2),
                       evac=("split" if c == 0 else "dve"))

        def inproj(c, name):
            b = next_bank()
            pos = BLKPOS[name] * 128
            xt = xT[c % 2]
            for kb in range(KB):
                P.op("pe", lambda e, b=b, pos=pos, kb=kb, xt=xt: e.matmul(
                    ps_all[:, b, :], lhsT=w_in_sb[:, kb, pos:pos + 128], rhs=xt[:, kb, :],
                    start=(kb == 0), stop=(kb == KB - 1)),
                    r=win_keys(name) + [("xT%d" % (c % 2), kb)], w=[pk(b)])
            return b

        def chunk_main(c):
            yt = YT[c % 2]
            ytk = "YT%d" % (c % 2)
            cur, prv = 0, 0
            for j in range(3):
                xp = XP[j][cur]
                xpp = XP[j][prv]
                P.op("act", lambda e, xp=xp, xpp=xpp: e.activation(out=xp[:, 0:16], in_=xpp[:, T:T + 16], func=AF.Copy),
                     r=[("XPtail", j, prv)], w=[("XPhist", j, cur)])
                u = U[j][cur]
                up = U[j][prv]
                P.op("act", lambda e, u=u, up=up: e.activation(out=u[:, 0:2], in_=up[:, T:T + 2], func=AF.Copy),
                     r=[("Utail", j, prv)], w=[("Uhist", j, cur)])
            for j in range(3):
                b = inproj(c, "xp%d" % j)
                xp = XP[j][cur]
                xpp = XP[j][prv]
                P.op("act", lambda e, b=b, xp=xp: e.activation(out=xp[:, 16:16 + T], in_=ps_all[:, b, :], func=AF.Copy),
                     r=[pk(b)], w=[("XPmain", j, cur), ("XPtail", j, cur)])
                xk = [("XPmain", j, cur), ("XPtail", j, cur), ("XPhist", j, cur)]
                W = T + 16
                ranges = POOL_RANGES[j]

                def legal(lo, hi):
                    out = []
                    while lo < hi:
                        if lo == 0:
                            nxt = hi
                        elif lo == 64:
                            nxt = min(hi, 128)
                        else:
                            nxt = min(hi, lo + 32)
                        out.append((lo, nxt))
                        lo = nxt
                    return out

                def qk(name, lo, hi):
                    return [(name, q) for q in range(lo // 32, hi // 32)]

                def finalize(src, sname, lo, hi, w, j=j, xp=xp, xk=xk):
                    P.op("dve", lambda e: e.scalar_tensor_tensor(out=diff_b[lo:hi, j, :], in0=src[lo:hi, 16:W],
                                                                 scalar=1.0 / w, in1=xp[lo:hi, 16:W],
                                                                 op0=ALU.mult, op1=ALU.subtract),
                         r=qk(sname, lo, hi) + xk, w=qk(("diff", j), lo, hi))
                    if c == 0:
                        P.op("dve", lambda e: e.tensor_tensor(out=fix_t[lo:hi, :], in0=src[lo:hi, 16:32],
                                                              in1=invcnt[lo:hi, j, :], op=ALU.mult),
                             r=qk(sname, lo, hi) + [("invcnt", j)], w=qk("fix_t", lo, hi))
                        P.op("dve", lambda e: e.tensor_tensor(out=diff_b[lo:hi, j, 0:16], in0=fix_t[lo:hi, :],
                                                              in1=xp[lo:hi, 16:32], op=ALU.subtract),
                             r=qk("fix_t", lo, hi) + xk, w=qk(("diff", j), lo, hi))

                P.op("dve", lambda e, xp=xp: e.tensor_tensor(out=sA[:, 1:W], in0=xp[:, 1:W], in1=xp[:, 0:W - 1], op=ALU.add),
                     r=xk, w=qk("sA", 0, 128))
                src, sname, dst, dname = sA, "sA", sB, "sB"
                lvl = 2
                while True:
                    for lo, hi, w_ in ranges:
                        if w_ == lvl:
                            finalize(src, sname, lo, hi, w_)
                    need = [(lo, hi) for lo, hi, w_ in ranges if w_ > lvl]
                    if not need:
                        break
                    nlo, nhi = min(r[0] for r in need), max(r[1] for r in need)
                    for lo, hi in legal(nlo, nhi):
                        P.op("dve", lambda e, src=src, dst=dst, lvl=lvl, lo=lo, hi=hi: e.tensor_tensor(
                            out=dst[lo:hi, 2 * lvl - 1:W], in0=src[lo:hi, 2 * lvl - 1:W], in1=src[lo:hi, lvl - 1:W - lvl],
                            op=ALU.add), r=qk(sname, lo, hi), w=qk(dname, lo, hi))
                    src, sname, dst, dname = dst, dname, src, sname
                    lvl *= 2
            if c == 0:
                kv_prep_t()
            for j in range(3):
                b = inproj(c, "gc%d" % j)
                P.op("act", lambda e, b=b, j=j: e.activation(out=sgc[j][:], in_=ps_all[:, b, :], func=AF.Silu),
                     r=[pk(b)], w=[("sgc", j)])
            if c == 0:
                kv_prep_b()
            for j in range(3):
                b = inproj(c, "gp%d" % j)
                P.op("act", lambda e, b=b, j=j: e.activation(out=sgp[j][:], in_=ps_all[:, b, :], func=AF.Silu),
                     r=[pk(b)], w=[("sgp", j)])
            for i in range(2):
                b = inproj(c, "ga%d" % i)
                P.op("act", lambda e, b=b, i=i: e.activation(out=sga[i][:], in_=ps_all[:, b, :], func=AF.Silu),
                     r=[pk(b)], w=[("sga", i)])
            P.op("act", lambda e: e.activation(out=dummy[:, 0:1], in_=eps_col[:], func=AF.Exp), r=["eps_col"], w=["dummy"])
            for i in range(2):
                b = inproj(c, "q%d" % i)
                P.op("act", lambda e, b=b, i=i: e.activation(out=qb[:, i, :], in_=ps_all[:, b, :], func=AF.Copy),
                     r=[pk(b)], w=[("qb", i)])
            scores(c)
            for j in range(3):
                if c == 0 and j == 1 and NCH > 1:
                    poolmix(0)
                    input_stage(1)
                u = U[j][cur]
                up = U[j][prv]
                xs = xcs[j % 2]
                tj = ta[j % 2]
                sg = sgc[j]
                b = inproj(c, "xc%d" % j)
                P.op("act", lambda e, b=b, xs=xs: e.activation(out=xs[:], in_=ps_all[:, b, :], func=AF.Copy),
                     r=[pk(b)], w=[("xcs", j % 2)])
                b = inproj(c, "cg%d" % j)
                P.op("dve", lambda e, b=b, xs=xs, u=u: e.tensor_tensor(out=u[:, 2:2 + T], in0=ps_all[:, b, :], in1=xs[:],
                                                                     op=ALU.mult),
                     r=[pk(b), ("xcs", j % 2)], w=[("Umain", j, cur), ("Utail", j, cur)])
                uk = [("Umain", j, cur), ("Utail", j, cur), ("Uhist", j, cur)]
                P.op("act", lambda e, u=u, tj=tj, j=j: e.activation(out=tj[:], in_=u[:, 2:2 + T], func=AF.Copy,
                                                                   scale=convw_col[:, j, 2:3]),
                     r=uk + [("convw", j)], w=[("ta", j % 2)])
                P.op("dve", lambda e, u=u, tj=tj, j=j: e.scalar_tensor_tensor(out=tj[:], in0=u[:, 1:1 + T],
                                                                            scalar=convw_col[:, j, 1:2], in1=tj[:],
                                                                            op0=ALU.mult, op1=ALU.add),
                     r=uk + [("convw", j), ("ta", j % 2)], w=[("ta", j % 2)])
                P.op("dve", lambda e, u=u, tj=tj, j=j: e.scalar_tensor_tensor(out=tj[:], in0=u[:, 0:T],
                                                                            scalar=convw_col[:, j, 0:1], in1=tj[:],
                                                                            op0=ALU.mult, op1=ALU.add),
                     r=uk + [("convw", j), ("ta", j % 2)], w=[("ta", j % 2)])
                b = inproj(c, "bg%d" % j)
                P.op("dve", lambda e, b=b, tj=tj: e.tensor_tensor(out=tj[:], in0=ps_all[:, b, :], in1=tj[:], op=ALU.mult),
                     r=[pk(b), ("ta", j % 2)], w=[("ta", j % 2)])
                P.op("pool", lambda e, tj=tj, sg=sg, j=j: e.tensor_tensor(out=yt[:, j, :], in0=tj[:], in1=sg[:], op=ALU.mult),
                     r=[("ta", j % 2), ("sgc", j)], w=[(ytk, j)])

        def scores(c):
            for p in range(2):
                for mb in range(2):
                    bs = [next_bank(), next_bank()]
                    for i in range(2):
                        b = bs[i]
                        P.op("pe", lambda e, b=b, p=p, mb=mb, i=i: e.matmul(
                            ps_all[:, b, :], lhsT=kT[i * 64:(i + 1) * 64, p, mb * 128:(mb + 1) * 128],
                            rhs=qb[i * 64:(i + 1) * 64, p, :], start=True, stop=True),
                            r=[("kT", p), ("qb", p)], w=[pk(b)])
                    for i in range(2):
                        b = bs[i]
                        ei = (p * 2 + i) * 2 + mb
                        P.op("act", lambda e, b=b, ei=ei: e.activation(out=E[:, ei, :], in_=ps_all[:, b, :], func=AF.Exp,
                                                                      scale=0.125),
                             r=[pk(b)], w=[("E", ei)])

        def poolmix(c):
            cur = c % 2
            yt = YT[cur]
            ytk = "YT%d" % cur
            for jo in range(3):
                b = next_bank()
                jis = [ji for (ji, jo_) in POOL_PAIRS if jo_ == jo]
                for n, ji in enumerate(jis):
                    P.op("pe", lambda e, b=b, jo=jo, ji=ji, n=n, last=len(jis) - 1: e.matmul(
                        ps_all[:, b, :], lhsT=wbig[:, ji, jo * 128:(jo + 1) * 128], rhs=diff_b[:, ji, :],
                        start=(n == 0), stop=(n == last)),
                        r=wbig_keys + ["wbig"] + [(("diff", ji), q) for q in range(4)], w=[pk(b)])
                P.op("dve", lambda e, b=b, jo=jo: e.scalar_tensor_tensor(
                    out=yt[:, 3 + jo, :], in0=ps_all[:, b, :], scalar=pscale_col[:, jo:jo + 1], in1=sgp[jo][:],
                    op0=ALU.mult, op1=ALU.mult),
                    r=[pk(b), "pscale", ("sgp", jo)], w=[(ytk, 3 + jo)])

        def attn_out(c):
            cur = c % 2
            yt = YT[cur]
            ytk = "YT%d" % cur
            for p in range(2):
                bo = next_bank()
                bd = next_bank()
                for i in range(2):
                    for mb in range(2):
                        ei = (p * 2 + i) * 2 + mb
                        h = p * 2 + i
                        P.op("pe", lambda e, bo=bo, i=i, mb=mb, ei=ei, h=h: e.matmul(
                            ps_all[i * 64:(i + 1) * 64, bo, :], lhsT=v_sb[:, mb, h * 64:(h + 1) * 64], rhs=E[:, ei, :],
                            start=(mb == 0), stop=(mb == 1)),
                            r=[("v", mb), ("E", ei)], w=[("psh", bo, i)] + ([pk(bo)] if (i == 0 and mb == 0) else []))
                for i in range(2):
                    for mb in range(2):
                        ei = (p * 2 + i) * 2 + mb
                        P.op("pe", lambda e, bd=bd, i=i, mb=mb, ei=ei: e.matmul(
                            ps_all[i * 64:(i + 1) * 64, bd, :], lhsT=ones_b[:, 0:64], rhs=E[:, ei, :],
                            start=(mb == 0), stop=(mb == 1)),
                            r=["ones_b", ("E", ei)], w=[("psh", bd, i)] + ([pk(bd)] if (i == 0 and mb == 0) else []))
                r_ = R[p]
                P.op("act", lambda e, bd=bd: e.activation(out=Ld[:], in_=ps_all[:, bd, :], func=AF.Ln),
                     r=[pk(bd), ("psh", bd, 0), ("psh", bd, 1)], w=["Ld"])
                P.op("act", lambda e, r_=r_: e.activation(out=r_[:], in_=Ld[:], func=AF.Exp, scale=-1.0),
                     r=["Ld"], w=[("R", p)])
                P.op("dve", lambda e, bo=bo, r_=r_: e.tensor_tensor(out=r_[:], in0=ps_all[:, bo, :], in1=r_[:], op=ALU.mult),
                     r=[pk(bo), ("psh", bo, 0), ("psh", bo, 1), ("R", p)], w=[("R", p)])
                P.op("pool", lambda e, r_=r_, p=p: e.tensor_tensor(out=yt[:, 6 + p, :], in0=r_[:], in1=sga[p][:], op=ALU.mult),
                     r=[("R", p), ("sga", p)], w=[(ytk, 6 + p)])

        def post_tiles(c, tiles):
            cur = c % 2
            yt = YT[cur]
            ytk = "YT%d" % cur
            banks = {}
            for i in tiles:
                b = next_bank_pair()
                banks[i] = b
                for half in range(2):
                    for kb in range(KB):
                        P.op("pe", lambda e, b=b, half=half, kb=kb, i=i: e.matmul(
                            ps_all[:, b + half, :], lhsT=yt[:, kb, i * 128:(i + 1) * 128],
                            rhs=w_out_sb[:, kb, half * 512:(half + 1) * 512], start=(kb == 0), stop=(kb == KB - 1)),
                            r=wout_keys(kb) + [(ytk, kb)], w=[pk(b + half)])
                yv = ps_all[:, b:b + 2, :]
                sumsq(lambda yv=yv: yv, [pk(b), pk(b + 1)], ss_y, i, "y")
            rstd_batch(ss_y, ln_y, rstd_y, tiles[0], tiles[-1] + 1, "y")
            for i in tiles:
                t = 4 * c + i
                slot = t % 4
                b = banks[i]
                yv = ps_all[:, b:b + 2, :]
                P.op("dve", lambda e, yv=yv, i=i, slot=slot: e.scalar_tensor_tensor(
                    out=tpost[slot][:].rearrange("p (a f) -> p a f", a=2), in0=yv, scalar=rstd_y[:, i:i + 1],
                    in1=postw_bc[:].rearrange("p (a f) -> p a f", a=2), op0=ALU.mult, op1=ALU.mult),
                    r=[pk(b), pk(b + 1), ("y", "rs", i), "postw_bc"], w=[("tpost", slot)])
                if t >= NT - 4:
                    xs_ = t % NXIN
                    P.op("dve", lambda e, slot=slot, xs_=xs_: e.tensor_tensor(out=tpost[slot][:], in0=tpost[slot][:],
                                                                             in1=xin[xs_][:], op=ALU.add),
                         r=[("tpost", slot), ("xin", xs_)], w=[("tpost", slot)])
                    P.op("sp", lambda e, t=t, slot=slot: e.dma_start(out=out_d[t * 128:(t + 1) * 128, :], in_=tpost[slot][:]),
                         r=[("tpost", slot)], chan="fin%d" % slot)
                else:
                    P.op("pool", lambda e, t=t, slot=slot: e.dma_start(out=out_d[t * 128:(t + 1) * 128, :],
                                                                      in_=tpost[slot][:], accum_op=ALU.add),
                         r=[("tpost", slot), ("out_t", t)], chan="acc%d" % slot)


        def post(c, tiles):
            for i in tiles:
                post_tiles(c, [i])

        input_stage(0)
        transpose_stage(0)
        kv_prep_a()
        load_wout(extra_deps=[("xin", s_) for s_ in range(NXIN)] if NT > 4 else ())
        for c in range(NCH + 1):
            if c < NCH:
                chunk_main(c)
                if c == 0 and NCH > 1:
                    transpose_stage(1)
                if not (c == 0 and NCH > 1):
                    poolmix(c)
                if c + 1 < NCH and c > 0:
                    input_stage(c + 1)
                attn_out(c)
            if c >= 1:
                post(c - 1, [0, 1])
            if c + 1 < NCH and c > 0:
                transpose_stage(c + 1)
            if c >= 1:
                post(c - 1, [2, 3])
            if c + 1 < NCH:
                P.op("act", lambda e: e.activation(out=dummy[:, 1:2], in_=eps_col[:], func=AF.Silu), r=["eps_col"], w=["dummy"])
        P.emit(st)
    return nc


_NC_CACHE = {}


def kernel(x, mem, pre_norm_w, mem_norm_w, w_in, conv_w, pool_w, pool_scale, w_kv, w_out, post_norm_w):
    x = np.asarray(x, dtype=np.float32)
    B, S, _ = x.shape
    if S not in _NC_CACHE:
        _NC_CACHE[S] = build_nc(S)
    nc = _NC_CACHE[S]
    f = lambda a: np.ascontiguousarray(np.asarray(a, dtype=np.float32))
    shared = {
        "pre_norm_w": f(pre_norm_w).reshape(D),
        "mem_norm_w": f(mem_norm_w).reshape(D),
        "w_in": f(w_in).reshape(D, DIN),
        "conv_w": f(conv_w).reshape(3, 384),
        "pool_w": f(pool_w).reshape(4, 96, 96),
        "pool_scale": f(pool_scale).reshape(384),
        "w_kv": f(w_kv).reshape(D, 512),
        "w_out": f(w_out).reshape(D, D),
        "post_norm_w": f(post_norm_w).reshape(1, D),
    }
    mem = np.asarray(mem, dtype=np.float32)
    in_maps = []
    for b in range(B):
        m = dict(shared)
        m["x"] = np.ascontiguousarray(x[b])
        m["mem"] = np.ascontiguousarray(mem[b])
        in_maps.append(m)
    res = run_bass_kernel_spmd(nc, in_maps, core_ids=list(range(B)))
    return np.stack([np.asarray(r["out"]) for r in res.results], axis=0).astype(np.float32)
```

```python
import numpy as np
from contextlib import ExitStack
import concourse.bass as bass
import concourse.mybir as mybir
from concourse.bass_utils import run_bass_kernel_spmd

F32 = mybir.dt.float32
BF16 = mybir.dt.bfloat16
I32 = mybir.dt.int32
AF = mybir.ActivationFunctionType
ALU = mybir.AluOpType

ENGS = ("pe", "act", "dve", "pool", "sp")


class Op:
    __slots__ = ("idx", "eng", "fn", "deps", "chan", "sig", "count", "sem", "name")


class Prog:
    def __init__(self, nc):
        self.nc = nc
        self.ops = []
        self.last_w = {}
        self.readers = {}

    def op(self, eng, fn, r=(), w=(), chan=None, name=None):
        o = Op()
        o.idx = len(self.ops)
        o.eng = eng
        o.fn = fn
        o.chan = chan
        o.sig = False
        o.count = None
        o.sem = None
        o.name = name
        deps = {}
        for k in r:
            p = self.last_w.get(k)
            if p is not None:
                deps[p] = "raw"
        for k in w:
            p = self.last_w.get(k)
            if p is not None and p not in deps:
                deps[p] = "waw"
            for rd in self.readers.get(k, ()):
                if rd not in deps:
                    deps[rd] = "war"
        o.deps = deps
        for k in r:
            self.readers.setdefault(k, []).append(o.idx)
        for k in w:
            self.last_w[k] = o.idx
            self.readers[k] = []
        self.ops.append(o)
        return o

    def _needs_wait(self, o, p, kind):
        if p.chan is not None:
            return True
        if p.eng != o.eng:
            return True
        if o.chan is not None:
            return True
        if o.eng == "pe":
            return False
        return True

    def emit(self, stack):
        nc = self.nc
        ops = self.ops
        for o in ops:
            for pi, kind in o.deps.items():
                p = ops[pi]
                if self._needs_wait(o, p, kind):
                    p.sig = True
        eng_sem = {e: stack.enter_context(nc.semaphore("s_" + e)) for e in ENGS}
        chan_sem = {}
        eng_cnt = {e: 0 for e in ENGS}
        chan_cnt = {}
        for o in ops:
            if o.chan is not None:
                if o.chan not in chan_sem:
                    chan_sem[o.chan] = stack.enter_context(nc.semaphore("c_" + o.chan))
                    chan_cnt[o.chan] = 0
                chan_cnt[o.chan] += 16
                o.sem = chan_sem[o.chan]
                o.count = chan_cnt[o.chan]
                o.sig = True
            elif o.sig:
                eng_cnt[o.eng] += 1
                o.sem = eng_sem[o.eng]
                o.count = eng_cnt[o.eng]
        self.n_sems = len(eng_sem) + len(chan_sem)
        block = stack.enter_context(nc.Block())

        def run(engname):
            def body(eng):
                waited = {}
                for o in ops:
                    if o.eng != engname:
                        continue
                    for pi, kind in o.deps.items():
                        p = ops[pi]
                        if not self._needs_wait(o, p, kind):
                            continue
                        key = id(p.sem)
                        if waited.get(key, 0) >= p.count:
                            continue
                        eng.wait_ge(p.sem, p.count)
                        waited[key] = p.count
                    ins = o.fn(eng)
                    if o.sig:
                        ins.then_inc(o.sem, 16 if o.chan is not None else 1)
                for o in ops:
                    if o.eng == engname and o.chan is not None:
                        key = id(o.sem)
                        if waited.get(key, 0) < o.count:
                            eng.wait_ge(o.sem, o.count)
                            waited[key] = o.count
            return body

        block.tensor(run("pe"))
        block.scalar(run("act"))
        block.vector(run("dve"))
        block.gpsimd(run("pool"))
        block.sync(run("sp"))


D = 1024
KB = 8
T = 512
NM = 256
DIN = 2816
EPS = 1e-6

def _colblocks():
    blocks = []
    for j in range(3):
        blocks.append(("xp%d" % j, [(1536 + 128 * j, 128)]))
    for j in range(3):
        blocks.append(("gc%d" % j, [(1152 + 128 * j, 128)]))
    for j in range(3):
        blocks.append(("gp%d" % j, [(1920 + 128 * j, 128)]))
    for i in range(2):
        blocks.append(("ga%d" % i, [(2560 + 128 * i, 128)]))
    for i in range(2):
        blocks.append(("q%d" % i, [(2304 + 128 * i, 128)]))
    for j in range(3):
        blocks.append(("xc%d" % j, [(0 + 128 * j, 128)]))
        blocks.append(("cg%d" % j, [(768 + 128 * j, 128)]))
        blocks.append(("bg%d" % j, [(384 + 128 * j, 128)]))
    return blocks


COLBLOCKS = _colblocks()
BLKPOS = {name: i for i, (name, _) in enumerate(COLBLOCKS)}
POOL_RANGES = [
    [(0, 96, 2), (96, 128, 4)],
    [(0, 64, 4), (64, 128, 8)],
    [(0, 32, 8), (32, 64, 16), (64, 128, 16)],
]


def _pool_pairs():
    pairs = set()
    for g in range(4):
        blks = sorted({r // 128 for r in range(96 * g, 96 * g + 96)})
        for a in blks:
            for b in blks:
                pairs.add((a, b))
    return sorted(pairs)


POOL_PAIRS = _pool_pairs()


def build_nc(S):
    NCH = S // T
    NT = S // 128
    nc = bass.Bass("TRN2", target_bir_lowering=False)
    x_d = nc.dram_tensor("x", [S, D], F32, kind="ExternalInput").ap()
    mem_d = nc.dram_tensor("mem", [NM, D], F32, kind="ExternalInput").ap()
    prew_d = nc.dram_tensor("pre_norm_w", [D], F32, kind="ExternalInput").ap()
    memw_d = nc.dram_tensor("mem_norm_w", [D], F32, kind="ExternalInput").ap()
    win_d = nc.dram_tensor("w_in", [D, DIN], F32, kind="ExternalInput").ap()
    convw_d = nc.dram_tensor("conv_w", [3, 384], F32, kind="ExternalInput").ap()
    poolw_d = nc.dram_tensor("pool_w", [4, 96, 96], F32, kind="ExternalInput").ap()
    pscale_d = nc.dram_tensor("pool_scale", [384], F32, kind="ExternalInput").ap()
    wkv_d = nc.dram_tensor("w_kv", [D, 512], F32, kind="ExternalInput").ap()
    wout_d = nc.dram_tensor("w_out", [D, D], F32, kind="ExternalInput").ap()
    postw_d = nc.dram_tensor("post_norm_w", [1, D], F32, kind="ExternalInput").ap()
    out_d = nc.dram_tensor("out", [S, D], F32, kind="ExternalOutput").ap()

    st = ExitStack()
    with st:
        def sb(name, shape, dt):
            return st.enter_context(nc.sbuf_tensor(name, shape, dt))

        w_in_sb = sb("w_in_sb", [128, KB, DIN], BF16)
        w_out_sb = sb("w_out_sb", [128, KB, D], BF16)
        wbig = sb("wbig", [128, 3, 384], BF16)
        kT = sb("kT", [128, 2, NM], BF16)
        v_sb = sb("v_sb", [128, 2, 256], BF16)
        ident_i = sb("ident_i", [128, 128], I32)
        ident_b = sb("ident_b", [128, 128], BF16)
        ones_b = sb("ones_b", [128, 64], BF16)
        eps_col = sb("eps_col", [128, 1], F32)
        prew_col = sb("prew_col", [128, KB], F32)
        memw_col = sb("memw_col", [128, KB], F32)
        convw_col = sb("convw_col", [128, 3, 3], F32)
        pscale_col = sb("pscale_col", [128, 3], F32)
        postw_bc = sb("postw_bc", [128, D], F32)
        wcol = sb("wcol", [128, 3], F32)
        iot_i = sb("iot_i", [128, 16], I32)
        iot_f = sb("iot_f", [128, 16], F32)
        cnt_f = sb("cnt_f", [128, 3, 16], F32)
        invcnt = sb("invcnt", [128, 3, 16], F32)
        fix_t = sb("fix_t", [128, 16], F32)
        dummy = sb("dummy_act", [128, 2], F32)

        NXIN, NHB = 4, 4
        xin = [sb("xin%d" % i, [128, D], F32) for i in range(NXIN)]
        hb = [sb("hb%d" % i, [128, D], BF16) for i in range(NHB + 2)]
        xT = [sb("xT%d" % i, [128, KB, T], BF16) for i in range(2)]
        YT = [sb("YT%d" % i, [128, KB, T], BF16) for i in range(2)]
        junks = [sb("junk%d" % i, [128, D], BF16) for i in range(2)]
        junk_ctr = [0]
        ss_in = sb("ss_in", [128, 4], F32)
        ln_in = sb("ln_in", [128, 4], F32)
        rstd_in = sb("rstd_in", [128, 4], F32)
        ss_y = sb("ss_y", [128, 4], F32)
        ln_y = sb("ln_y", [128, 4], F32)
        rstd_y = sb("rstd_y", [128, 4], F32)
        xcs = [sb("xcs%d" % i, [128, T], F32) for i in range(2)]
        U = [[sb("U%d_%d" % (j, p), [128, T + 2], F32) for p in range(1)] for j in range(3)]
        ta = [sb("ta%d" % i, [128, T], F32) for i in range(2)]
        sgc = [sb("sgc%d" % i, [128, T], F32) for i in range(3)]
        sgp = [sb("sgp%d" % i, [128, T], BF16) for i in range(3)]
        sga = [sb("sga%d" % i, [128, T], BF16) for i in range(2)]
        XP = [[sb("XP%d_%d" % (j, p), [128, T + 16], F32) for p in range(1)] for j in range(3)]
        sA = sb("sA", [128, T + 16], F32)
        sB = sb("sB", [128, T + 16], F32)
        diff_b = sb("diff_b", [128, 3, T], BF16)
        qb = sb("qb", [128, 2, T], BF16)
        E = sb("E", [128, 8, T], BF16)
        Ld = sb("Ld", [128, T], F32)
        R = [sb("R%d" % i, [128, T], F32) for i in range(2)]
        tpost = [sb("tpost%d" % i, [128, D], F32) for i in range(4)]
        ps_all = st.enter_context(nc.psum_tensor("ps_all", [128, 8, 512], F32))

        P = Prog(nc)
        bank_ctr = [0]

        def next_bank():
            b = bank_ctr[0] % 8
            bank_ctr[0] += 1
            return b

        def next_bank_pair():
            if bank_ctr[0] % 8 == 7:
                bank_ctr[0] += 1
            b = bank_ctr[0] % 8
            bank_ctr[0] += 2
            return b

        def pk(b):
            return ("ps", b)

        P.op("pool", lambda e: e.iota(ident_i[:], pattern=[[1, 128]], base=0, channel_multiplier=-1),
             w=["ident_i"])
        P.op("dve", lambda e: e.tensor_scalar(out=ident_b[:], in0=ident_i[:], scalar1=0.0, scalar2=None,
                                              op0=ALU.is_equal), r=["ident_i"], w=["ident_b"])
        P.op("dve", lambda e: e.memset(ones_b[:], 1.0), w=["ones_b"])
        P.op("dve", lambda e: e.memset(eps_col[:], EPS), w=["eps_col"])
        P.op("dve", lambda e: e.memset(wbig[:], 0.0), w=["wbig"])
        P.op("act", lambda e: e.activation(out=ln_in[:, 0:1], in_=eps_col[:], func=AF.Exp), r=["eps_col"], w=[("in", "ln", 0)])
        for j in range(3):
            for ri, (lo, hi, w_) in enumerate(POOL_RANGES[j]):
                P.op("dve", lambda e, j=j, lo=lo, hi=hi, w_=w_: e.memset(wcol[lo:hi, j:j + 1], float(w_)),
                     w=[("wcol", j, ri)])
            P.op("pool", lambda e, j=j: e.memset(XP[j][0][:, T:T + 16], 0.0), w=[("XPtail", j, 0)])
            P.op("pool", lambda e, j=j: e.memset(U[j][0][:, T:T + 2], 0.0), w=[("Utail", j, 0)])
        P.op("pool", lambda e: e.iota(iot_i[:], pattern=[[1, 16]], base=1, channel_multiplier=0), w=["iot_i"])
        P.op("dve", lambda e: e.tensor_copy(out=iot_f[:], in_=iot_i[:]), r=["iot_i"], w=["iot_f"])
        for j in range(3):
            P.op("dve", lambda e, j=j: e.tensor_scalar(out=cnt_f[:, j, :], in0=iot_f[:], scalar1=wcol[:, j:j + 1],
                                                       scalar2=None, op0=ALU.min),
                 r=["iot_f"] + [("wcol", j, ri) for ri in range(len(POOL_RANGES[j]))], w=[("cnt_f", j)])
            P.op("dve", lambda e, j=j: e.reciprocal(out=invcnt[:, j, :], in_=cnt_f[:, j, :]),
                 r=[("cnt_f", j)], w=[("invcnt", j)])

        def load_xin(t, extra_deps=()):
            slot = t % NXIN
            P.op("sp", lambda e, t=t, slot=slot: e.dma_start(out=xin[slot][:], in_=x_d[t * 128:(t + 1) * 128, :]),
                 r=list(extra_deps), w=[("xin", slot)], chan="xin%d" % slot)

        for t in range(4):
            load_xin(t)

        def small(e, out, in_):
            with nc.allow_non_contiguous_dma(reason="tiny parameter load"):
                return e.dma_start(out=out, in_=in_)

        P.op("sp", lambda e: small(e, prew_col[:], prew_d.rearrange("(kb p) -> p kb", p=128)),
             w=["prew_col"], chan="prew")
        P.op("sp", lambda e: small(e, memw_col[:], memw_d.rearrange("(kb p) -> p kb", p=128)),
             w=["memw_col"], chan="memw")
        for j in range(3):
            P.op("sp", lambda e, j=j: small(e, convw_col[:, j, :],
                                            convw_d[:, j * 128:(j + 1) * 128].rearrange("k p -> p k")),
                 w=[("convw", j)], chan="convw%d" % j)
        P.op("sp", lambda e: small(e, pscale_col[:], pscale_d.rearrange("(j p) -> p j", p=128)),
             w=["pscale"], chan="psc")
        P.op("sp", lambda e: e.dma_start(out=postw_bc[:], in_=postw_d.partition_broadcast(128)),
             w=["postw_bc"], chan="postw")

        win_v = win_d.rearrange("(kb p) c -> p kb c", p=128)
        wkv_v = wkv_d.rearrange("(kb p) c -> p kb c", p=128)
        wout_v = wout_d.rearrange("(kb p) c -> p kb c", p=128)

        def load_win(name):
            pos = BLKPOS[name] * 128
            off = 0
            for ri, (a, n) in enumerate(COLBLOCKS[BLKPOS[name]][1]):
                P.op("pool", lambda e, a=a, n=n, o=pos + off: e.dma_start(out=w_in_sb[:, :, o:o + n],
                                                                         in_=win_v[:, :, a:a + n]),
                     w=[("win", name, ri)], chan="win_%s_%d" % (name, ri))
                off += n

        def win_keys(name):
            return [("win", name, ri) for ri in range(len(COLBLOCKS[BLKPOS[name]][1]))]

        for i in range(2):
            P.op("pool", lambda e, i=i: e.dma_start(out=tpost[i][:], in_=mem_d[i * 128:(i + 1) * 128, :]),
                 w=[("tpost", i)], chan="memld%d" % i)
        for name, _ in COLBLOCKS[:3]:
            load_win(name)
        P.op("pool", lambda e: e.dma_start(out=E[:], in_=wkv_v), w=[("E", i) for i in range(8)], chan="wkv")
        wbig_keys = []
        for g in range(4):
            for ji in range(3):
                r0, r1 = max(96 * g, 128 * ji), min(96 * g + 96, 128 * ji + 128)
                if r0 >= r1:
                    continue
                P.op("pool", lambda e, g=g, ji=ji, r0=r0, r1=r1: e.dma_start(
                    out=wbig[r0 - 128 * ji:r1 - 128 * ji, ji, 96 * g:96 * g + 96],
                    in_=poolw_d[g, r0 - 96 * g:r1 - 96 * g, :]),
                    r=["wbig"], w=[("wbigd", g, ji)], chan="pw%d_%d" % (g, ji))
                wbig_keys.append(("wbigd", g, ji))
        for name, _ in COLBLOCKS[3:]:
            load_win(name)
        wout_rows = [[(128 * j, 128)] for j in range(3)]
        wout_rows += [[(384 + 128 * j, 128)] for j in range(3)]
        wout_rows += [[(768 + 128 * i, 128)] for i in range(2)]
        def load_wout(extra_deps=()):
            for kb in range(KB):
                off = 0
                for ri, (a, n) in enumerate(wout_rows[kb]):
                    P.op("pool", lambda e, kb=kb, a=a, n=n, off=off: e.dma_start(out=w_out_sb[off:off + n, kb, :],
                                                                                in_=wout_d[a:a + n, :]),
                         r=list(extra_deps), w=[("wout", kb, ri)], chan="wout%d_%d" % (kb, ri))
                    off += n

        def wout_keys(kb):
            return [("wout", kb, ri) for ri in range(len(wout_rows[kb]))]


        def sumsq(src_fn, src_keys, ssb, i, tag):
            jn = junk_ctr[0] % 2
            junk_ctr[0] += 1
            junk = junks[jn]
            P.op("act", lambda e: e.activation(out=(junk[:].rearrange("p (a f) -> p a f", a=2) if len(src_fn().shape) == 3 else junk[:]), in_=src_fn(), func=AF.Square, accum_out=ssb[:, i:i + 1]),
                 r=src_keys, w=[("junk", jn), (tag, "ss", i)])

        def rstd_batch(ssb, lnb, rsb, i0, i1, tag):
            P.op("act", lambda e: e.activation(out=lnb[:, i0:i1], in_=ssb[:, i0:i1], func=AF.Ln,
                                               bias=eps_col[:], scale=1.0 / D),
                 r=[(tag, "ss", i) for i in range(i0, i1)] + ["eps_col"], w=[(tag, "ln", i) for i in range(i0, i1)])
            P.op("act", lambda e: e.activation(out=rsb[:, i0:i1], in_=lnb[:, i0:i1], func=AF.Exp, scale=-0.5),
                 r=[(tag, "ln", i) for i in range(i0, i1)], w=[(tag, "rs", i) for i in range(i0, i1)])

        def transposes(hb_idx, ncols, scal_col, scal_key, dst, dst_key, evac="dve"):
            nt = len(hb_idx)
            bs = [next_bank() for _ in range(4)]
            bvs = [ps_all[:, b, :].bitcast(BF16) for b in bs]
            for ii, hi in enumerate(hb_idx):
                for kb in range(KB):
                    pr, kk = kb // 2, kb % 2
                    P.op("pe", lambda e, bv=bvs[pr], kk=kk, ii=ii, hi=hi, kb=kb: e.transpose(
                        out=bv[:, kk * 512 + ii * 128:kk * 512 + (ii + 1) * 128],
                        in_=hb[hi][:, kb * 128:(kb + 1) * 128], identity=ident_b[:]),
                        r=[("hb", hi), "ident_b"], w=[pk(bs[pr])])
            for kb in range(KB):
                pr, kk = kb // 2, kb % 2
                b = bs[pr]
                bv = bvs[pr]
                if evac == "dve" or (evac == "split" and pr < 2):
                    P.op("dve", lambda e, bv=bv, kk=kk, kb=kb: e.tensor_scalar(
                        out=dst[:, kb, 0:nt * 128], in0=bv[:, kk * 512:kk * 512 + nt * 128],
                        scalar1=scal_col[:, kb:kb + 1], scalar2=None, op0=ALU.mult),
                        r=[pk(b), scal_key], w=[(dst_key, kb)])
                else:
                    P.op("act", lambda e, bv=bv, kk=kk, kb=kb: e.activation(
                        out=dst[:, kb, 0:nt * 128], in_=bv[:, kk * 512:kk * 512 + nt * 128], func=AF.Copy,
                        scale=scal_col[:, kb:kb + 1]),
                        r=[pk(b), scal_key], w=[(dst_key, kb)])

        def kv_prep_a():
            for i in range(2):
                sumsq(lambda i=i: tpost[i][:], [("tpost", i)], ss_y, i, "y")
            rstd_batch(ss_y, ln_y, rstd_y, 0, 2, "y")
            for i in range(2):
                P.op("dve", lambda e, i=i: e.tensor_scalar(out=hb[NHB + i][:], in0=tpost[i][:], scalar1=rstd_y[:, i:i + 1],
                                                           scalar2=None, op0=ALU.mult),
                     r=[("tpost", i), ("y", "rs", i)], w=[("hb", NHB + i)])

        def kv_prep_t():
            transposes([NHB, NHB + 1], 2, memw_col, "memw_col", YT[1], "YT1", evac="act")

        def kv_prep_b():
            memT = YT[1]
            for p in range(2):
                b = next_bank()
                for kb in range(KB):
                    P.op("pe", lambda e, b=b, p=p, kb=kb: e.matmul(ps_all[:, b, 0:NM], lhsT=E[:, kb, p * 128:(p + 1) * 128],
                                                                  rhs=memT[:, kb, 0:NM], start=(kb == 0), stop=(kb == KB - 1)),
                         r=[("E", kb), ("YT1", kb)], w=[pk(b)])
                P.op("act", lambda e, b=b, p=p: e.activation(out=kT[:, p, :], in_=ps_all[:, b, 0:NM], func=AF.Copy),
                     r=[pk(b)], w=[("kT", p)])
            for mb in range(2):
                b = next_bank()
                for kb in range(KB):
                    P.op("pe", lambda e, b=b, mb=mb, kb=kb: e.matmul(ps_all[:, b, 0:256], lhsT=memT[:, kb, mb * 128:(mb + 1) * 128],
                                                                    rhs=E[:, kb, 256:512], start=(kb == 0), stop=(kb == KB - 1)),
                         r=[("E", kb), ("YT1", kb)], w=[pk(b)])
                P.op("act", lambda e, b=b, mb=mb: e.activation(out=v_sb[:, mb, :], in_=ps_all[:, b, 0:256], func=AF.Copy),
                     r=[pk(b)], w=[("v", mb)])


        def input_stage(c):
            for i in range(4):
                t = 4 * c + i
                slot = t % NXIN
                sumsq(lambda slot=slot: xin[slot][:], [("xin", slot)], ss_in, i, "in")
                if c == 0:
                    rstd_batch(ss_in, ln_in, rstd_in, i, i + 1, "in")
            if c > 0:
                rstd_batch(ss_in, ln_in, rstd_in, 0, 4, "in")
            for i in range(4):
                t = 4 * c + i
                slot = t % NXIN
                P.op("dve", lambda e, slot=slot, i=i: e.tensor_scalar(out=hb[i][:], in0=xin[slot][:],
                                                                      scalar1=rstd_in[:, i:i + 1], scalar2=None,
                                                                      op0=ALU.mult),
                     r=[("xin", slot), ("in", "rs", i)], w=[("hb", i)])
                if t < NT - 4 and c > 0:
                    P.op("sp", lambda e, t=t, slot=slot: e.dma_start(out=out_d[t * 128:(t + 1) * 128, :], in_=xin[slot][:]),
                         r=[("xin", slot)], w=[("out_t", t)], chan="xst%d" % slot)
                if t + NXIN < NT:
                    load_xin(t + NXIN, extra_deps=(win_keys("q1") if c == 0 else ()))
            if c == 0 and NT > 4:
                for i in range(4):
                    P.op("sp", lambda e, i=i: e.dma_start(out=out_d[i * 128:(i + 1) * 128, :], in_=x_d[i * 128:(i + 1) * 128, :]),
                         r=win_keys(COLBLOCKS[-1][0]) + [("xin", s_) for s_ in range(NXIN)], w=[("out_t", i)],
                         chan="cp0_%d" % i)

        def transpose_stage(c):
            transposes([0, 1, 2, 3], 4, prew_col, "prew_col", xT[c % 2], "xT%d" % (c % 2),
                       evac=("split" if c == 0 else "dve"))

        def inproj(c, name):
            b = next_bank()
            pos = BLKPOS[name] * 128
            xt = xT[c % 2]
            for kb in range(KB):
                P.op("pe", lambda e, b=b, pos=pos, kb=kb, xt=xt: e.matmul(
                    ps_all[:, b, :], lhsT=w_in_sb[:, kb, pos:pos + 128], rhs=xt[:, kb, :],
                    start=(kb == 0), stop=(kb == KB - 1)),
                    r=win_keys(name) + [("xT%d" % (c % 2), kb)], w=[pk(b)])
            return b

        def chunk_main(c):
            yt = YT[c % 2]
            ytk = "YT%d" % (c % 2)
            cur, prv = 0, 0
            for j in range(3):
                xp = XP[j][cur]
                xpp = XP[j][prv]
                P.op("act", lambda e, xp=xp, xpp=xpp: e.activation(out=xp[:, 0:16], in_=xpp[:, T:T + 16], func=AF.Copy),
                     r=[("XPtail", j, prv)], w=[("XPhist", j, cur)])
                u = U[j][cur]
                up = U[j][prv]
                P.op("act", lambda e, u=u, up=up: e.activation(out=u[:, 0:2], in_=up[:, T:T + 2], func=AF.Copy),
                     r=[("Utail", j, prv)], w=[("Uhist", j, cur)])
            for j in range(3):
                b = inproj(c, "xp%d" % j)
                xp = XP[j][cur]
                xpp = XP[j][prv]
                P.op("act", lambda e, b=b, xp=xp: e.activation(out=xp[:, 16:16 + T], in_=ps_all[:, b, :], func=AF.Copy),
                     r=[pk(b)], w=[("XPmain", j, cur), ("XPtail", j, cur)])
                xk = [("XPmain", j, cur), ("XPtail", j, cur), ("XPhist", j, cur)]
                W = T + 16
                ranges = POOL_RANGES[j]

                def legal(lo, hi):
                    out = []
                    while lo < hi:
                        if lo == 0:
                            nxt = hi
                        elif lo == 64:
                            nxt = min(hi, 128)
                        else:
                            nxt = min(hi, lo + 32)
                        out.append((lo, nxt))
                        lo = nxt
                    return out

                def qk(name, lo, hi):
                    return [(name, q) for q in range(lo // 32, hi // 32)]

                def finalize(src, sname, lo, hi, w, j=j, xp=xp, xk=xk):
                    P.op("dve", lambda e: e.scalar_tensor_tensor(out=diff_b[lo:hi, j, :], in0=src[lo:hi, 16:W],
                                                                 scalar=1.0 / w, in1=xp[lo:hi, 16:W],
                                                                 op0=ALU.mult, op1=ALU.subtract),
                         r=qk(sname, lo, hi) + xk, w=qk(("diff", j), lo, hi))
                    if c == 0:
                        P.op("dve", lambda e: e.tensor_tensor(out=fix_t[lo:hi, :], in0=src[lo:hi, 16:32],
                                                              in1=invcnt[lo:hi, j, :], op=ALU.mult),
                             r=qk(sname, lo, hi) + [("invcnt", j)], w=qk("fix_t", lo, hi))
                        P.op("dve", lambda e: e.tensor_tensor(out=diff_b[lo:hi, j, 0:16], in0=fix_t[lo:hi, :],
                                                              in1=xp[lo:hi, 16:32], op=ALU.subtract),
                             r=qk("fix_t", lo, hi) + xk, w=qk(("diff", j), lo, hi))

                P.op("dve", lambda e, xp=xp: e.tensor_tensor(out=sA[:, 1:W], in0=xp[:, 1:W], in1=xp[:, 0:W - 1], op=ALU.add),
                     r=xk, w=qk("sA", 0, 128))
                src, sname, dst, dname = sA, "sA", sB, "sB"
                lvl = 2
                while True:
                    for lo, hi, w_ in ranges:
                        if w_ == lvl:
                            finalize(src, sname, lo, hi, w_)
                    need = [(lo, hi) for lo, hi, w_ in ranges if w_ > lvl]
                    if not need:
                        break
                    nlo, nhi = min(r[0] for r in need), max(r[1] for r in need)
                    for lo, hi in legal(nlo, nhi):
                        P.op("dve", lambda e, src=src, dst=dst, lvl=lvl, lo=lo, hi=hi: e.tensor_tensor(
                            out=dst[lo:hi, 2 * lvl - 1:W], in0=src[lo:hi, 2 * lvl - 1:W], in1=src[lo:hi, lvl - 1:W - lvl],
                            op=ALU.add), r=qk(sname, lo, hi), w=qk(dname, lo, hi))
                    src, sname, dst, dname = dst, dname, src, sname
                    lvl *= 2
            if c == 0:
                kv_prep_t()
            for j in range(3):
                b = inproj(c, "gc%d" % j)
                P.op("act", lambda e, b=b, j=j: e.activation(out=sgc[j][:], in_=ps_all[:, b, :], func=AF.Silu),
                     r=[pk(b)], w=[("sgc", j)])
            if c == 0:
                kv_prep_b()
            for j in range(3):
                b = inproj(c, "gp%d" % j)
                P.op("act", lambda e, b=b, j=j: e.activation(out=sgp[j][:], in_=ps_all[:, b, :], func=AF.Silu),
                     r=[pk(b)], w=[("sgp", j)])
            for i in range(2):
                b = inproj(c, "ga%d" % i)
                P.op("act", lambda e, b=b, i=i: e.activation(out=sga[i][:], in_=ps_all[:, b, :], func=AF.Silu),
                     r=[pk(b)], w=[("sga", i)])
            P.op("act", lambda e: e.activation(out=dummy[:, 0:1], in_=eps_col[:], func=AF.Exp), r=["eps_col"], w=["dummy"])
            for i in range(2):
                b = inproj(c, "q%d" % i)
                P.op("act", lambda e, b=b, i=i: e.activation(out=qb[:, i, :], in_=ps_all[:, b, :], func=AF.Copy),
                     r=[pk(b)], w=[("qb", i)])
            scores(c)
            for j in range(3):
                if c == 0 and j == 1 and NCH > 1:
                    poolmix(0)
                    input_stage(1)
                u = U[j][cur]
                up = U[j][prv]
                xs = xcs[j % 2]
                tj = ta[j % 2]
                sg = sgc[j]
                b = inproj(c, "xc%d" % j)
                P.op("act", lambda e, b=b, xs=xs: e.activation(out=xs[:], in_=ps_all[:, b, :], func=AF.Copy),
                     r=[pk(b)], w=[("xcs", j % 2)])
                b = inproj(c, "cg%d" % j)
                P.op("dve", lambda e, b=b, xs=xs, u=u: e.tensor_tensor(out=u[:, 2:2 + T], in0=ps_all[:, b, :], in1=xs[:],
                                                                     op=ALU.mult),
                     r=[pk(b), ("xcs", j % 2)], w=[("Umain", j, cur), ("Utail", j, cur)])
                uk = [("Umain", j, cur), ("Utail", j, cur), ("Uhist", j, cur)]
                P.op("act", lambda e, u=u, tj=tj, j=j: e.activation(out=tj[:], in_=u[:, 2:2 + T], func=AF.Copy,
                                                                   scale=convw_col[:, j, 2:3]),
                     r=uk + [("convw", j)], w=[("ta", j % 2)])
                P.op("dve", lambda e, u=u, tj=tj, j=j: e.scalar_tensor_tensor(out=tj[:], in0=u[:, 1:1 + T],
                                                                            scalar=convw_col[:, j, 1:2], in1=tj[:],
                                                                            op0=ALU.mult, op1=ALU.add),
                     r=uk + [("convw", j), ("ta", j % 2)], w=[("ta", j % 2)])
                P.op("dve", lambda e, u=u, tj=tj, j=j: e.scalar_tensor_tensor(out=tj[:], in0=u[:, 0:T],
                                                                            scalar=convw_col[:, j, 0:1], in1=tj[:],
                                                                            op0=ALU.mult, op1=ALU.add),
                     r=uk + [("convw", j), ("ta", j % 2)], w=[("ta", j % 2)])
                b = inproj(c, "bg%d" % j)
                P.op("dve", lambda e, b=b, tj=tj: e.tensor_tensor(out=tj[:], in0=ps_all[:, b, :], in1=tj[:], op=ALU.mult),
                     r=[pk(b), ("ta", j % 2)], w=[("ta", j % 2)])
                P.op("pool", lambda e, tj=tj, sg=sg, j=j: e.tensor_tensor(out=yt[:, j, :], in0=tj[:], in1=sg[:], op=ALU.mult),
                     r=[("ta", j % 2), ("sgc", j)], w=[(ytk, j)])

        def scores(c):
            for p in range(2):
                for mb in range(2):
                    bs = [next_bank(), next_bank()]
                    for i in range(2):
                        b = bs[i]
                        P.op("pe", lambda e, b=b, p=p, mb=mb, i=i: e.matmul(
                            ps_all[:, b, :], lhsT=kT[i * 64:(i + 1) * 64, p, mb * 128:(mb + 1) * 128],
                            rhs=qb[i * 64:(i + 1) * 64, p, :], start=True, stop=True),
                            r=[("kT", p), ("qb", p)], w=[pk(b)])
                    for i in range(2):
                        b = bs[i]
                        ei = (p * 2 + i) * 2 + mb
                        P.op("act", lambda e, b=b, ei=ei: e.activation(out=E[:, ei, :], in_=ps_all[:, b, :], func=AF.Exp,
                                                                      scale=0.125),
                             r=[pk(b)], w=[("E", ei)])

        def poolmix(c):
            cur = c % 2
            yt = YT[cur]
            ytk = "YT%d" % cur
            for jo in range(3):
                b = next_bank()
                jis = [ji for (ji, jo_) in POOL_PAIRS if jo_ == jo]
                for n, ji in enumerate(jis):
                    P.op("pe", lambda e, b=b, jo=jo, ji=ji, n=n, last=len(jis) - 1: e.matmul(
                        ps_all[:, b, :], lhsT=wbig[:, ji, jo * 128:(jo + 1) * 128], rhs=diff_b[:, ji, :],
                        start=(n == 0), stop=(n == last)),
                        r=wbig_keys + ["wbig"] + [(("diff", ji), q) for q in range(4)], w=[pk(b)])
                P.op("dve", lambda e, b=b, jo=jo: e.scalar_tensor_tensor(
                    out=yt[:, 3 + jo, :], in0=ps_all[:, b, :], scalar=pscale_col[:, jo:jo + 1], in1=sgp[jo][:],
                    op0=ALU.mult, op1=ALU.mult),
                    r=[pk(b), "pscale", ("sgp", jo)], w=[(ytk, 3 + jo)])

        def attn_out(c):
            cur = c % 2
            yt = YT[cur]
            ytk = "YT%d" % cur
            for p in range(2):
                bo = next_bank()
                bd = next_bank()
                for i in range(2):
                    for mb in range(2):
                        ei = (p * 2 + i) * 2 + mb
                        h = p * 2 + i
                        P.op("pe", lambda e, bo=bo, i=i, mb=mb, ei=ei, h=h: e.matmul(
                            ps_all[i * 64:(i + 1) * 64, bo, :], lhsT=v_sb[:, mb, h * 64:(h + 1) * 64], rhs=E[:, ei, :],
                            start=(mb == 0), stop=(mb == 1)),
                            r=[("v", mb), ("E", ei)], w=[("psh", bo, i)] + ([pk(bo)] if (i == 0 and mb == 0) else []))
                for i in range(2):
                    for mb in range(2):
                        ei = (p * 2 + i) * 2 + mb
                        P.op("pe", lambda e, bd=bd, i=i, mb=mb, ei=ei: e.matmul(
                            ps_all[i * 64:(i + 1) * 64, bd, :], lhsT=ones_b[:, 0:64], rhs=E[:, ei, :],
                            start=(mb == 0), stop=(mb == 1)),
                            r=["ones_b", ("E", ei)], w=[("psh", bd, i)] + ([pk(bd)] if (i == 0 and mb == 0) else []))
                r_ = R[p]
                P.op("act", lambda e, bd=bd: e.activation(out=Ld[:], in_=ps_all[:, bd, :], func=AF.Ln),
                     r=[pk(bd), ("psh", bd, 0), ("psh", bd, 1)], w=["Ld"])
                P.op("act", lambda e, r_=r_: e.activation(out=r_[:], in_=Ld[:], func=AF.Exp, scale=-1.0),
                     r=["Ld"], w=[("R", p)])
                P.op("dve", lambda e, bo=bo, r_=r_: e.tensor_tensor(out=r_[:], in0=ps_all[:, bo, :], in1=r_[:], op=ALU.mult),
                     r=[pk(bo), ("psh", bo, 0), ("psh", bo, 1), ("R", p)], w=[("R", p)])
                P.op("pool", lambda e, r_=r_, p=p: e.tensor_tensor(out=yt[:, 6 + p, :], in0=r_[:], in1=sga[p][:], op=ALU.mult),
                     r=[("R", p), ("sga", p)], w=[(ytk, 6 + p)])

        def post_tiles(c, tiles):
            cur = c % 2
            yt = YT[cur]
            ytk = "YT%d" % cur
            banks = {}
            for i in tiles:
                b = next_bank_pair()
                banks[i] = b
                for half in range(2):
                    for kb in range(KB):
                        P.op("pe", lambda e, b=b, half=half, kb=kb, i=i: e.matmul(
                            ps_all[:, b + half, :], lhsT=yt[:, kb, i * 128:(i + 1) * 128],
                            rhs=w_out_sb[:, kb, half * 512:(half + 1) * 512], start=(kb == 0), stop=(kb == KB - 1)),
                            r=wout_keys(kb) + [(ytk, kb)], w=[pk(b + half)])
                yv = ps_all[:, b:b + 2, :]
                sumsq(lambda yv=yv: yv, [pk(b), pk(b + 1)], ss_y, i, "y")
            rstd_batch(ss_y, ln_y, rstd_y, tiles[0], tiles[-1] + 1, "y")
            for i in tiles:
                t = 4 * c + i
                slot = t % 4
                b = banks[i]
                yv = ps_all[:, b:b + 2, :]
                P.op("dve", lambda e, yv=yv, i=i, slot=slot: e.scalar_tensor_tensor(
                    out=tpost[slot][:].rearrange("p (a f) -> p a f", a=2), in0=yv, scalar=rstd_y[:, i:i + 1],
                    in1=postw_bc[:].rearrange("p (a f) -> p a f", a=2), op0=ALU.mult, op1=ALU.mult),
                    r=[pk(b), pk(b + 1), ("y", "rs", i), "postw_bc"], w=[("tpost", slot)])
                if t >= NT - 4:
                    xs_ = t % NXIN
                    P.op("dve", lambda e, slot=slot, xs_=xs_: e.tensor_tensor(out=tpost[slot][:], in0=tpost[slot][:],
                                                                             in1=xin[xs_][:], op=ALU.add),
                         r=[("tpost", slot), ("xin", xs_)], w=[("tpost", slot)])
                    P.op("sp", lambda e, t=t, slot=slot: e.dma_start(out=out_d[t * 128:(t + 1) * 128, :], in_=tpost[slot][:]),
                         r=[("tpost", slot)], chan="fin%d" % slot)
                else:
                    P.op("pool", lambda e, t=t, slot=slot: e.dma_start(out=out_d[t * 128:(t + 1) * 128, :],
                                                                      in_=tpost[slot][:], accum_op=ALU.add),
                         r=[("tpost", slot), ("out_t", t)], chan="acc%d" % slot)


        def post(c, tiles):
            for i in tiles:
                post_tiles(c, [i])

        input_stage(0)
        transpose_stage(0)
        kv_prep_a()
        load_wout(extra_deps=[("xin", s_) for s_ in range(NXIN)] if NT > 4 else ())
        for c in range(NCH + 1):
            if c < NCH:
                chunk_main(c)
                if c == 0 and NCH > 1:
                    transpose_stage(1)
                if not (c == 0 and NCH > 1):
                    poolmix(c)
                if c + 1 < NCH and c > 0:
                    input_stage(c + 1)
                attn_out(c)
            if c >= 1:
                post(c - 1, [0, 1])
            if c + 1 < NCH and c > 0:
                transpose_stage(c + 1)
            if c >= 1:
                post(c - 1, [2, 3])
            if c + 1 < NCH:
                P.op("act", lambda e: e.activation(out=dummy[:, 1:2], in_=eps_col[:], func=AF.Silu), r=["eps_col"], w=["dummy"])
        P.emit(st)
    return nc


_NC_CACHE = {}


def kernel(x, mem, pre_norm_w, mem_norm_w, w_in, conv_w, pool_w, pool_scale, w_kv, w_out, post_norm_w):
    x = np.asarray(x, dtype=np.float32)
    B, S, _ = x.shape
    if S not in _NC_CACHE:
        _NC_CACHE[S] = build_nc(S)
    nc = _NC_CACHE[S]
    f = lambda a: np.ascontiguousarray(np.asarray(a, dtype=np.float32))
    shared = {
        "pre_norm_w": f(pre_norm_w).reshape(D),
        "mem_norm_w": f(mem_norm_w).reshape(D),
        "w_in": f(w_in).reshape(D, DIN),
        "conv_w": f(conv_w).reshape(3, 384),
        "pool_w": f(pool_w).reshape(4, 96, 96),
        "pool_scale": f(pool_scale).reshape(384),
        "w_kv": f(w_kv).reshape(D, 512),
        "w_out": f(w_out).reshape(D, D),
        "post_norm_w": f(post_norm_w).reshape(1, D),
    }
    mem = np.asarray(mem, dtype=np.float32)
    in_maps = []
    for b in range(B):
        m = dict(shared)
        m["x"] = np.ascontiguousarray(x[b])
        m["mem"] = np.ascontiguousarray(mem[b])
        in_maps.append(m)
    res = run_bass_kernel_spmd(nc, in_maps, core_ids=list(range(B)))
    return np.stack([np.asarray(r["out"]) for r in res.results], axis=0).astype(np.float32)
```
